# Optimizing a Trainium2 kernel written in Bass

```python
import math
import jax, jax.numpy as jnp
from jax import lax
import numpy as np

D_MODEL = 1024
BATCH = 16
SEQ = 4096
DEPTH = 4

EXPAND = 2
D_INNER = EXPAND * D_MODEL
CONV_WIDTH = 31
CHUNK = 128
SGU_GROUPS = 8
PLE_DIM = 256
N_MIXERS = 2
N_CONV_LAYERS = (DEPTH + 1) // 2
N_SGU_LAYERS = DEPTH // 2
EPS = 1e-6

kernel_name = "hybrid_conformer_conv_gmlp_trunk"


def _rmsnorm(x, g):
    xf = x.astype(jnp.float32)
    y = xf * lax.rsqrt(jnp.mean(xf * xf, axis=-1, keepdims=True) + EPS)
    return (y * g.astype(jnp.float32)).astype(x.dtype)


def _layernorm(x, g, b):
    xf = x.astype(jnp.float32)
    mu = jnp.mean(xf, axis=-1, keepdims=True)
    xc = xf - mu
    var = jnp.mean(xc * xc, axis=-1, keepdims=True)
    y = xc * lax.rsqrt(var + EPS)
    return (y * g.astype(jnp.float32) + b.astype(jnp.float32)).astype(x.dtype)


def _causal_depthwise_conv(x, w, b):
    k = w.shape[0]
    y = lax.conv_general_dilated(
        x, w[:, None, :].astype(x.dtype),
        window_strides=(1,), padding=[(k - 1, 0)],
        dimension_numbers=("NWC", "WIO", "NWC"),
        feature_group_count=x.shape[-1])
    return y + b.astype(x.dtype)


def _conformer_conv_mixer(a, b_gate, conv_w, conv_b, ln_g, ln_b):
    y = a * jax.nn.sigmoid(b_gate)
    y = _causal_depthwise_conv(y, conv_w, conv_b)
    y = _layernorm(y, ln_g, ln_b)
    return jax.nn.silu(y)


def _chunked_sgu_mixer(a, b_half, ln_g, ln_b, w_s, b_s):
    bsz, seq, e = a.shape
    n_chunks = seq // CHUNK
    u = jax.nn.gelu(a, approximate=False)
    v = _layernorm(jax.nn.gelu(b_half, approximate=False), ln_g, ln_b)
    vg = v.reshape(bsz, n_chunks, CHUNK, SGU_GROUPS, e // SGU_GROUPS)
    mask = jnp.tril(jnp.ones((CHUNK, CHUNK), dtype=bool))
    w = jnp.where(mask[None], w_s, jnp.zeros((), w_s.dtype)).astype(v.dtype)
    mixed = jnp.einsum("gts,bnsgc->bntgc", w, vg)
    mixed = mixed + b_s.T.astype(v.dtype)[None, None, :, :, None]
    return u * mixed.reshape(bsz, seq, e)


def setup_inputs(seed: int = 0) -> dict:
    key = jax.random.key(seed)
    ks = jax.random.split(key, 20)
    f32 = jnp.float32
    E = D_INNER
    x = jax.random.normal(ks[0], (BATCH, SEQ, D_MODEL), f32)
    p = jax.random.normal(ks[1], (DEPTH, BATCH, SEQ, PLE_DIM), f32)
    norm_g = 1.0 + 0.02 * jax.random.normal(ks[2], (DEPTH, D_MODEL), f32)
    w_in = jax.random.normal(ks[3], (DEPTH, D_MODEL, 3 * E), f32) * D_MODEL ** -0.5
    w_out = jax.random.normal(ks[4], (DEPTH, E, D_MODEL), f32) * E ** -0.5
    conv_w = jax.random.normal(ks[5], (N_CONV_LAYERS, CONV_WIDTH, E), f32) * CONV_WIDTH ** -0.5
    conv_b = 0.02 * jax.random.normal(ks[6], (N_CONV_LAYERS, E), f32)
    conv_ln_g = 1.0 + 0.02 * jax.random.normal(ks[7], (N_CONV_LAYERS, E), f32)
    conv_ln_b = 0.02 * jax.random.normal(ks[8], (N_CONV_LAYERS, E), f32)
    sgu_ln_g = 1.0 + 0.02 * jax.random.normal(ks[9], (N_SGU_LAYERS, E), f32)
    sgu_ln_b = 0.02 * jax.random.normal(ks[10], (N_SGU_LAYERS, E), f32)
    sgu_w = jax.random.normal(ks[11], (N_SGU_LAYERS, SGU_GROUPS, CHUNK, CHUNK), f32) * CHUNK ** -0.5
    sgu_b = 1.0 + 0.1 * jax.random.normal(ks[12], (N_SGU_LAYERS, SGU_GROUPS, CHUNK), f32)
    pl_norm_g = 1.0 + 0.02 * jax.random.normal(ks[13], (DEPTH, D_MODEL), f32)
    pl_gate_w = jax.random.normal(ks[14], (DEPTH, D_MODEL, D_MODEL), f32) * D_MODEL ** -0.5
    pl_proj_w = jax.random.normal(ks[15], (DEPTH, PLE_DIM, D_MODEL), f32) * PLE_DIM ** -0.5
    final_g = 1.0 + 0.02 * jax.random.normal(ks[16], (D_MODEL,), f32)
    return {"x": x, "p": p, "norm_g": norm_g, "w_in": w_in, "w_out": w_out,
            "conv_w": conv_w, "conv_b": conv_b, "conv_ln_g": conv_ln_g, "conv_ln_b": conv_ln_b,
            "sgu_ln_g": sgu_ln_g, "sgu_ln_b": sgu_ln_b, "sgu_w": sgu_w, "sgu_b": sgu_b,
            "pl_norm_g": pl_norm_g, "pl_gate_w": pl_gate_w, "pl_proj_w": pl_proj_w,
            "final_g": final_g}


def reference(x, p, norm_g, w_in, w_out, conv_w, conv_b, conv_ln_g, conv_ln_b,
              sgu_ln_g, sgu_ln_b, sgu_w, sgu_b, pl_norm_g, pl_gate_w, pl_proj_w, final_g):
    for i in range(DEPTH):
        h = _rmsnorm(x, norm_g[i])
        proj = jnp.einsum("bsd,de->bse", h, w_in[i])
        a, b_half, z = jnp.split(proj, 3, axis=-1)
        j = i // N_MIXERS
        if i % N_MIXERS == 0:
            y = _conformer_conv_mixer(a, b_half, conv_w[j], conv_b[j], conv_ln_g[j], conv_ln_b[j])
        else:
            y = _chunked_sgu_mixer(a, b_half, sgu_ln_g[j], sgu_ln_b[j], sgu_w[j], sgu_b[j])
        x = x + jnp.einsum("bse,ed->bsd", y * jax.nn.silu(z), w_out[i])
        gate = jax.nn.sigmoid(jnp.einsum("bsd,de->bse", _rmsnorm(x, pl_norm_g[i]), pl_gate_w[i]))
        x = x + gate * jnp.einsum("bsk,kd->bsd", p[i], pl_proj_w[i])
    return _rmsnorm(x, final_g)
```

```python
import numpy as np
import concourse.bass as bass
import concourse.mybir as mybir
from concourse.bass_utils import run_bass_kernel_spmd

F32 = mybir.dt.float32
BF16 = mybir.dt.bfloat16
AF = mybir.ActivationFunctionType
ALU = mybir.AluOpType

D = 1024
E = 2048
DEPTH = 4
KW = 31
PLE = 256
NCORES = 8
SEQ = 4096
T = 512
EPS = 1e-6
KC = D // 128
EC = E // 128
HALO = KW - 1
KD = 8
DTAPS = [2 * i for i in range(KD)]
PTAPS = [k for k in range(KW) if k not in DTAPS]

BLK = 128 * 1024
PP = 128 * 2048


def _win_pieces(l):
    if l % 2 == 0:
        p1 = [[(0, c), (1, c)] for c in range(EC)]
        p2 = [[(2, c)] for c in range(EC)]
    else:
        p1 = [[(1, c)] for c in range(EC)]
        p2 = [[(0, c), (2, c)] for c in range(EC)]
    return p1, p2


def _weight_offsets():
    off = 0
    win = {}
    for l in range(DEPTH):
        p1, p2 = _win_pieces(l)
        for ph, pcs in ((1, p1), (2, p2)):
            for c, blks in enumerate(pcs):
                win[(l, ph, c)] = (off, len(blks))
                off += BLK * len(blks)
    wout = {}
    for l in range(DEPTH):
        for dc in range(KC):
            wout[(l, dc)] = off
            off += 128 * EC * 128
    wg = {}
    for l in range(DEPTH):
        for dc in range(KC):
            wg[(l, dc)] = off
            off += 128 * KC * 128
    wp = {}
    for l in range(DEPTH):
        wp[l] = off
        off += 128 * 2 * D
    wsgu = off
    off += 128 * 2 * 8 * 128
    assert off % PP == 0
    return win, wout, wg, wp, wsgu, off


WIN_OFF, WOUT_OFF, WG_OFF, WP_OFF, WSGU_OFF, NW = _weight_offsets()

PRM = {}
_o = 0
for _n, _sz in (("ng", DEPTH * KC), ("png", DEPTH * KC), ("fg", KC), ("cw", 2 * EC * KW),
                ("cb", 2 * EC), ("clg", 2 * EC), ("clb", 2 * EC), ("slg", 2 * EC), ("slb", 2 * EC)):
    PRM[_n] = _o
    _o += _sz
NPRM = _o


def _host_weights(w_in, w_out, pl_gate_w, pl_proj_w, sgu_w):
    parts = []
    for l in range(DEPTH):
        wl = w_in[l].reshape(KC, 128, 3, EC, 128)
        p1, p2 = _win_pieces(l)
        for pcs in (p1, p2):
            for blks in pcs:
                bl = [wl[:, :, j, c, :].transpose(1, 0, 2).reshape(128, 1024) for (j, c) in blks]
                parts.append(np.ascontiguousarray(np.stack(bl, axis=1)).reshape(-1))
    for l in range(DEPTH):
        wl = w_out[l].reshape(EC, 128, KC, 128)
        parts.append(np.ascontiguousarray(wl.transpose(2, 1, 0, 3)).reshape(-1))
    for l in range(DEPTH):
        wl = pl_gate_w[l].reshape(KC, 128, KC, 128)
        parts.append(np.ascontiguousarray(wl.transpose(2, 1, 0, 3)).reshape(-1))
    for l in range(DEPTH):
        wl = pl_proj_w[l].reshape(2, 128, D)
        parts.append(np.ascontiguousarray(wl.transpose(1, 0, 2)).reshape(-1))
    parts.append(np.ascontiguousarray(sgu_w.transpose(3, 0, 1, 2)).reshape(-1))
    flat = np.concatenate(parts).astype(np.float32, copy=False)
    assert flat.size == NW
    return flat


def _host_params(norm_g, pl_norm_g, final_g, conv_w, conv_b, conv_ln_g, conv_ln_b, sgu_ln_g, sgu_ln_b):
    prm = np.zeros((128, NPRM), np.float32)

    def put(name, arr):
        prm[:, PRM[name]:PRM[name] + arr.shape[1]] = arr

    put("ng", norm_g.reshape(DEPTH, KC, 128).transpose(2, 0, 1).reshape(128, -1))
    put("png", pl_norm_g.reshape(DEPTH, KC, 128).transpose(2, 0, 1).reshape(128, -1))
    put("fg", final_g.reshape(KC, 128).T)
    put("cw", conv_w.reshape(2, KW, EC, 128).transpose(3, 0, 2, 1).reshape(128, -1))
    for nm, a in (("cb", conv_b), ("clg", conv_ln_g), ("clb", conv_ln_b), ("slg", sgu_ln_g), ("slb", sgu_ln_b)):
        put(nm, a.reshape(2, EC, 128).transpose(2, 0, 1).reshape(128, -1))
    return prm


class Op:
    __slots__ = ("eng", "fn", "deps", "signal", "is_dma", "dma_key", "sig")

    def __init__(self, eng, fn, is_dma, dma_key):
        self.eng = eng
        self.fn = fn
        self.deps = []
        self.signal = is_dma
        self.is_dma = is_dma
        self.dma_key = dma_key
        self.sig = None


class Sched:
    ENGS = ("pe", "act", "dve", "pool", "sp")

    def __init__(self):
        self.ops = {e: [] for e in self.ENGS}
        self.last_w = {}
        self.readers = {}
        self.frozen = set()
        self.dma_keys = []

    def add(self, eng, fn, reads=(), writes=(), dma_key=None):
        is_dma = dma_key is not None
        op = Op(eng, fn, is_dma, dma_key)
        if is_dma and dma_key not in self.dma_keys:
            self.dma_keys.append(dma_key)
        deps = {}
        for k in reads:
            w = self.last_w.get(k)
            if w is not None:
                deps[id(w)] = w
        for k in writes:
            assert k not in self.frozen, k
            w = self.last_w.get(k)
            if w is not None:
                deps[id(w)] = w
            for r in self.readers.get(k, ()):
                deps[id(r)] = r
        for d in deps.values():
            if d.eng == "pe" and eng == "pe" and not d.is_dma and not is_dma:
                continue
            op.deps.append(d)
            d.signal = True
        for k in reads:
            if k not in self.frozen:
                self.readers.setdefault(k, []).append(op)
        for k in writes:
            self.last_w[k] = op
            self.readers[k] = []
        self.ops[eng].append(op)
        return op

    def freeze(self, *keys):
        self.frozen.update(keys)

    def emit(self, nc, final_wait_keys):
        import contextlib
        with contextlib.ExitStack() as es:
            esem = {e: es.enter_context(nc.semaphore("s_" + e)) for e in self.ENGS}
            dsem = {k: es.enter_context(nc.semaphore("d_" + "".join(ch for ch in str(k) if ch.isalnum()))) for k in self.dma_keys}
            cnt = {e: 0 for e in self.ENGS}
            dcnt = {k: 0 for k in self.dma_keys}
            for e in self.ENGS:
                for op in self.ops[e]:
                    if op.is_dma:
                        dcnt[op.dma_key] += 16
                        op.sig = (dsem[op.dma_key], dcnt[op.dma_key])
                    elif op.signal:
                        cnt[e] += 1
                        op.sig = (esem[e], cnt[e])
            block = es.enter_context(nc.Block())

            def run(engname, engobj, final=False):
                waited = {}
                for op in self.ops[engname]:
                    need = {}
                    for d in op.deps:
                        s, v = d.sig
                        if waited.get(id(s), 0) >= v:
                            continue
                        if need.get(id(s), (None, 0))[1] < v:
                            need[id(s)] = (s, v)
                    for s, v in need.values():
                        engobj.wait_ge(s, v)
                        waited[id(s)] = v
                    ins = op.fn(engobj)
                    if op.sig is not None:
                        ins.then_inc(op.sig[0], 16 if op.is_dma else 1)
                if final:
                    for k in final_wait_keys:
                        engobj.wait_ge(dsem[k], dcnt[k])

            @block.tensor
            def _(e):
                run("pe", e)

            @block.scalar
            def _(e):
                run("act", e)

            @block.vector
            def _(e):
                run("dve", e)

            @block.gpsimd
            def _(e):
                run("pool", e)

            @block.sync
            def _(e):
                run("sp", e, final=True)


class Stream:
    def __init__(self, sch, name, nslots, pieces, issue):
        self.sch, self.name, self.n, self.pieces, self.issue = sch, name, nslots, pieces, issue
        self.next = 0

    def get(self, i):
        hi = min(len(self.pieces), i + self.n)
        while self.next < hi:
            j = self.next
            self.issue(j, j % self.n, self.pieces[j])
            self.next += 1
        return i % self.n


def build_program(ntiles, depth=DEPTH, tiles_per_seq=SEQ // T, do_prepass=True):
    ntok = ntiles * T
    nc = bass.Bass("TRN2", target_bir_lowering=False)
    x_d = nc.dram_tensor("x_t", [KC, 128, ntok], F32, kind="ExternalInput").ap()
    p_d = nc.dram_tensor("p_t", [DEPTH, 2, 128, ntok], F32, kind="ExternalInput").ap()
    w_d = nc.dram_tensor("wts", [NW // PP, 128, 2048], F32, kind="ExternalInput").ap()
    prm_d = nc.dram_tensor("prm", [128, NPRM], F32, kind="ExternalInput").ap()
    sgb_d = nc.dram_tensor("sgb", [1, 2 * 8 * 128], F32, kind="ExternalInput").ap()
    out_d = nc.dram_tensor("out_t", [KC, 128, ntok], F32, kind="ExternalOutput").ap()
    wsc = nc.dram_tensor("wsc", [NW], BF16, kind="Internal").ap()
    wsc_pp = wsc.rearrange("(n p f) -> n p f", p=128, f=2048)

    import contextlib
    es = contextlib.ExitStack()

    def sb(name, shape, dt):
        return es.enter_context(nc.sbuf_tensor(name, shape, dt))

    NWIN = 4
    NF = 12
    xb = sb("xb", [128, KC, T], F32)
    hb = sb("hb", [128, KC, T], BF16)
    cbuf = sb("cbuf", [128, EC, T], F32)
    ub = sb("ub", [128, EC, T], BF16)
    ybuf = sb("ybuf", [128, 2, T + HALO], BF16)
    halo = sb("halo", [128, 2, EC, HALO], BF16)
    diag = sb("diag", [128, 2, KW, 128], BF16)
    fs = sb("fs", [128, NF, T], F32)
    bs = sb("bs", [128, 8, T], BF16)
    st = sb("st", [128, 4, T], F32)
    pbuf = sb("pbuf", [128, 2, T], F32)
    pb16 = sb("pb16", [128, 2, T], BF16)
    win = sb("win", [128, NWIN, 2048], BF16)
    wout = sb("wout", [128, 2, EC * 128], BF16)
    wgb = sb("wgb", [128, 2, KC * 128], BF16)
    wpb = sb("wpb", [128, 2, 2 * D], BF16)
    wtb = sb("wtb", [128, 2 * 8 * 128], BF16)
    bbc = sb("bbc", [128, 2 * 8 * 128], F32)
    prm = sb("prm_sb", [128, NPRM], F32)
    ident = sb("ident", [128, 128], BF16)
    onesD = sb("onesD", [128, 128], BF16)
    onesE = sb("onesE", [128, 128], BF16)
    onesf = sb("onesf", [128, 128], F32)
    epsb = sb("epsb", [128, 1], F32)
    ps = [es.enter_context(nc.psum_tensor("ps%d" % i, [128, T], F32)) for i in range(8)]

    sch = Sched()
    add = sch.add

    def prm_col(name, idx):
        o = PRM[name] + idx
        return prm[:, o:o + 1]

    add("sp", lambda e: e.dma_start(out=prm[:], in_=prm_d), writes=["prm"], dma_key="setup")
    add("sp", lambda e: e.dma_start(out=bbc[:], in_=sgb_d.partition_broadcast(128)), writes=["bbc"], dma_key="setup2")
    add("pool", lambda e: e.memset(onesf[:], 1.0), writes=["onesf"])
    add("pool", lambda e: e.memset(onesD[:], 1.0 / D), writes=["onesD"])
    add("pool", lambda e: e.memset(epsb[:], EPS), writes=["epsb"])
    add("pool", lambda e: e.memset(onesE[:], 1.0 / E), writes=["onesE"])
    add("pool", lambda e: e.affine_select(out=ident[:], in_=onesf[:], pattern=[[-1, 128]],
                                          compare_op=ALU.is_equal, fill=0.0, base=0, channel_multiplier=1),
        reads=["onesf"], writes=["ident"])
    sch.freeze("prm", "bbc", "onesD", "onesE", "ident", "onesf", "epsb")

    cst = cbuf[:].rearrange("p (s c) t -> p s (c t)", s=4)
    ust = ub[:].rearrange("p (s c) t -> p s (c t)", s=4)
    npp = NW // PP
    if do_prepass:
        cast_engs = ("act", "dve")
        for i in range(npp):
            s = i % 4
            ckeys = [("cbuf", 4 * s + q) for q in range(4)]
            ukeys = [("ub", 4 * s + q) for q in range(4)]
            add("sp", lambda e, i=i, s=s: e.dma_start(out=cst[:, s, :], in_=w_d[i]),
                writes=ckeys, dma_key=("ppl", s))
            ce = cast_engs[i % 2]
            if ce == "act":
                add("act", lambda e, s=s: e.copy(out=ust[:, s, :], in_=cst[:, s, :]), reads=ckeys, writes=ukeys)
            else:
                add(ce, lambda e, s=s: e.tensor_copy(out=ust[:, s, :], in_=cst[:, s, :]), reads=ckeys, writes=ukeys)
            add("sp", lambda e, i=i, s=s: e.dma_start(out=wsc_pp[i], in_=ust[:, s, :]),
                reads=ukeys, writes=[("wsc", i)], dma_key=("pps", s))
    all_wsc = [("wsc", i) for i in range(npp)]

    def wsc_view(off, n):
        return wsc[off:off + n].rearrange("(p f) -> p f", p=128)

    add("sp", lambda e: e.dma_start(out=wtb[:], in_=wsc_view(WSGU_OFF, 128 * 2048)),
        reads=all_wsc, writes=["wtb"], dma_key="setup3")
    wtb3 = wtb[:].rearrange("p (a t) -> p a t", t=128)
    add("pool", lambda e: e.affine_select(out=wtb3, in_=wtb3, pattern=[[0, 16], [1, 128]],
                                          compare_op=ALU.is_ge, fill=0.0, base=0, channel_multiplier=-1),
        reads=["wtb"], writes=["wtb"])
    sch.freeze("wtb")

    win_pieces, wout_pieces, wg_pieces, wp_pieces, p_pieces = [], [], [], [], []
    for ti in range(ntiles):
        for l in range(depth):
            for ph in (1, 2):
                for c in range(EC):
                    win_pieces.append(WIN_OFF[(l, ph, c)])
            for dc in range(KC):
                wout_pieces.append(WOUT_OFF[(l, dc)])
                wg_pieces.append(WG_OFF[(l, dc)])
            wp_pieces.append(WP_OFF[l])
            p_pieces.append((ti, l))

    def issue_win(j, s, pc):
        off, nb = pc
        add("sp", lambda e: e.dma_start(out=win[:, s, 0:nb * 1024], in_=wsc_view(off, BLK * nb)),
            reads=all_wsc if j < NWIN else (), writes=[("win", s)], dma_key=("win", s))

    def issue_wout(j, s, off):
        add("sp", lambda e: e.dma_start(out=wout[:, s, :], in_=wsc_view(off, 128 * 2048)),
            reads=all_wsc if j < 2 else (), writes=[("wout", s)], dma_key=("wout", s))

    def issue_wg(j, s, off):
        add("sp", lambda e: e.dma_start(out=wgb[:, s, :], in_=wsc_view(off, 128 * 1024)),
            reads=all_wsc if j < 2 else (), writes=[("wg", s)], dma_key=("wg", s))

    def issue_wp(j, s, off):
        add("sp", lambda e: e.dma_start(out=wpb[:, s, :], in_=wsc_view(off, 128 * 2048)),
            reads=all_wsc if j < 2 else (), writes=[("wp", s)], dma_key=("wp", s))

    def issue_p(j, s, pc):
        ti, l = pc
        add("sp", lambda e: e.dma_start(out=pbuf[:],
                                        in_=p_d[l, :, :, ti * T:(ti + 1) * T].rearrange("k p t -> p k t")),
            writes=["pbuf"], dma_key="pld")

    S_win = Stream(sch, "win", NWIN, win_pieces, issue_win)
    S_wout = Stream(sch, "wout", 2, wout_pieces, issue_wout)
    S_wg = Stream(sch, "wg", 2, wg_pieces, issue_wg)
    S_wp = Stream(sch, "wp", 2, wp_pieces, issue_wp)
    S_p = Stream(sch, "p", 1, p_pieces, issue_p)

    rot = {"f": 0, "fl": 0, "b": 0, "y": 0, "dg": 0, "psm": 0, "psc": 0}

    NFL = 6

    def fnew():
        i = NFL + rot["f"] % (NF - NFL)
        rot["f"] += 1
        return i

    def fnew_long():
        i = rot["fl"] % NFL
        rot["fl"] += 1
        return i

    def bnew():
        i = rot["b"] % 8
        rot["b"] += 1
        return i

    def psm_new():
        i = rot["psm"] % 4
        rot["psm"] += 1
        return i

    def psc_new():
        i = 4 + rot["psc"] % 2
        rot["psc"] += 1
        return i

    PS_SUM, PS_SQ = 6, 7

    def mm(out, lhsT, rhs, start, stop, reads, bank):
        add("pe", lambda e: e.matmul(out, lhsT, rhs, start=start, stop=stop), reads=reads, writes=[("ps", bank)])

    rms_state = {"n": 0, "pend": None}

    def rms_flush():
        pd = rms_state["pend"]
        if pd is not None:
            b, first, last = pd
            mm(ps[PS_SUM][:], onesD[:], bs[:, b, :], first, last, [("bs", b), "onesD"], PS_SUM)
            rms_state["pend"] = None

    def rms_partial(kc):
        rms_flush()
        b = bnew()
        add("act", lambda e: e.activation(out=bs[:, b, :], in_=xb[:, kc, :], func=AF.Square),
            reads=[("x", kc)], writes=[("bs", b)])
        n = rms_state["n"]
        rms_state["pend"] = (b, n == 0, n == KC - 1)
        rms_state["n"] = (n + 1) % KC

    def rms_finalize():
        rms_flush()
        assert rms_state["n"] == 0
        add("act", lambda e: e.activation(out=st[:, 2, :], in_=ps[PS_SUM][:], func=AF.Ln, bias=epsb[:], scale=1.0),
            reads=[("ps", PS_SUM), "epsb"], writes=[("st", 2)])
        add("act", lambda e: e.activation(out=st[:, 2, :], in_=st[:, 2, :], func=AF.Exp, scale=-0.5),
            reads=[("st", 2)], writes=[("st", 2)])

    def h_from_x(gname, l):
        rms_finalize()
        for kc in range(KC):
            eng = "dve"
            add(eng, lambda e, kc=kc: e.scalar_tensor_tensor(
                out=hb[:, kc, :], in0=xb[:, kc, :], scalar=prm_col(gname, l * KC + kc), in1=st[:, 2, :],
                op0=ALU.mult, op1=ALU.mult),
                reads=[("x", kc), ("st", 2), "prm"], writes=[("h", kc)])

    def proj_block(bank, slot, blk):
        for kc in range(KC):
            o = blk * 1024 + kc * 128
            mm(ps[bank][:], win[:, slot, o:o + 128], hb[:, kc, :], kc == 0, kc == KC - 1,
               [("h", kc), ("win", slot)], bank)

    def pipeline(n, stages):
        ns = len(stages)
        for i in range(n + ns - 1):
            for k, stg in enumerate(stages):
                c = i - k
                if 0 <= c < n:
                    stg(c)

    def layer(ti, l, widx, last_layer):
        j = l // 2
        conv = (l % 2 == 0)
        seq_start = (ti % tiles_per_seq == 0)
        h_from_x("ng", l)
        cst_ = [dict() for _ in range(EC)]

        def stats_prep(c):
            b1, b2 = bnew(), bnew()
            add("act", lambda e: e.activation(out=bs[:, b1, :], in_=cbuf[:, c, :], func=AF.Square),
                reads=[("cbuf", c)], writes=[("bs", b1)])
            add("dve", lambda e: e.tensor_copy(out=bs[:, b2, :], in_=cbuf[:, c, :]),
                reads=[("cbuf", c)], writes=[("bs", b2)])
            cst_[c]["b1"], cst_[c]["b2"] = b1, b2

        def stats_mm(c):
            b1, b2 = cst_[c]["b1"], cst_[c]["b2"]
            mm(ps[PS_SUM][:], onesE[:], bs[:, b2, :], c == 0, c == EC - 1, [("bs", b2), "onesE"], PS_SUM)
            mm(ps[PS_SQ][:], onesE[:], bs[:, b1, :], c == 0, c == EC - 1, [("bs", b1), "onesE"], PS_SQ)

        if conv:
            def sA(c):
                slot = S_win.get(widx[0])
                widx[0] += 1
                ba, bb_ = psm_new(), psm_new()
                proj_block(ba, slot, 0)
                proj_block(bb_, slot, 1)
                f = fnew()
                add("act", lambda e: e.activation(out=fs[:, f, :], in_=ps[bb_][:], func=AF.Sigmoid),
                    reads=[("ps", bb_)], writes=[("fs", f)])
                y = rot["y"] % 2
                rot["y"] += 1
                if seq_start:
                    add("pool", lambda e: e.memset(ybuf[:, y, 0:HALO], 0.0), writes=[("yh", y)])
                else:
                    add("pool", lambda e: e.tensor_copy(out=ybuf[:, y, 0:HALO], in_=halo[:, j, c, :]),
                        reads=[("halo", j, c)], writes=[("yh", y)])
                add("dve", lambda e: e.tensor_tensor(out=ybuf[:, y, HALO:HALO + T], in0=ps[ba][:],
                                                     in1=fs[:, f, :], op=ALU.mult),
                    reads=[("ps", ba), ("fs", f)], writes=[("y", y)])
                add("pool", lambda e: e.tensor_copy(out=halo[:, j, c, :], in_=ybuf[:, y, T:T + HALO]),
                    reads=[("y", y)], writes=[("halo", j, c)])
                dg = rot["dg"] % 2
                rot["dg"] += 1
                cwo = PRM["cw"] + (j * EC + c) * KW
                add("pool", lambda e: e.affine_select(
                    out=diag[:, dg, :, :], in_=prm[:, cwo:cwo + KW].unsqueeze(2).to_broadcast([128, KW, 128]),
                    pattern=[[0, KW], [-1, 128]], compare_op=ALU.is_equal, fill=0.0, base=0, channel_multiplier=1),
                    reads=["prm"], writes=[("diag", dg)])
                cst_[c]["y"], cst_[c]["dg"] = y, dg

            def sB(c):
                y, dg = cst_[c]["y"], cst_[c]["dg"]
                bc = psc_new()
                for k in PTAPS:
                    mm(ps[bc][:], diag[:, dg, k, :], ybuf[:, y, k:k + T], k == PTAPS[0], k == PTAPS[-1],
                       [("diag", dg), ("y", y), ("yh", y)], bc)
                cwo = PRM["cw"] + (j * EC + c) * KW
                if KD > 0:
                    fa = fnew()
                    k0 = DTAPS[0]
                    add("dve", lambda e: e.tensor_scalar(out=fs[:, fa, :], in0=ybuf[:, y, k0:k0 + T], scalar1=prm[:, cwo + k0:cwo + k0 + 1],
                                                         scalar2=None, op0=ALU.mult),
                        reads=[("y", y), ("yh", y), "prm"], writes=[("fs", fa)])
                    for k in DTAPS[1:]:
                        add("dve", lambda e, k=k: e.scalar_tensor_tensor(
                            out=fs[:, fa, :], in0=ybuf[:, y, k:k + T], scalar=prm[:, cwo + k:cwo + k + 1], in1=fs[:, fa, :],
                            op0=ALU.mult, op1=ALU.add),
                            reads=[("y", y), ("yh", y), "prm", ("fs", fa)], writes=[("fs", fa)])
                    add("dve", lambda e: e.scalar_tensor_tensor(
                        out=cbuf[:, c, :], in0=ps[bc][:], scalar=prm_col("cb", j * EC + c), in1=fs[:, fa, :],
                        op0=ALU.add, op1=ALU.add),
                        reads=[("ps", bc), "prm", ("fs", fa)], writes=[("cbuf", c)])
                else:
                    add("act", lambda e: e.activation(out=cbuf[:, c, :], in_=ps[bc][:], func=AF.Identity,
                                                      bias=prm_col("cb", j * EC + c), scale=1.0),
                        reads=[("ps", bc), "prm"], writes=[("cbuf", c)])
                stats_prep(c)

            pipeline(EC, [sA, sB, stats_mm])
        else:
            def sA(c):
                slot = S_win.get(widx[0])
                widx[0] += 1
                bb_ = psm_new()
                proj_block(bb_, slot, 0)
                add("act", lambda e: e.activation(out=cbuf[:, c, :], in_=ps[bb_][:], func=AF.Gelu),
                    reads=[("ps", bb_)], writes=[("cbuf", c)])
                stats_prep(c)

            pipeline(EC, [sA, stats_mm])
        add("act", lambda e: e.activation(out=st[:, 3, :], in_=ps[PS_SUM][:], func=AF.Square),
            reads=[("ps", PS_SUM)], writes=[("st", 3)])
        add("dve", lambda e: e.tensor_tensor(out=st[:, 1, :], in0=ps[PS_SQ][:], in1=st[:, 3, :], op=ALU.subtract),
            reads=[("ps", PS_SQ), ("st", 3)], writes=[("st", 1)])
        add("act", lambda e: e.activation(out=st[:, 1, :], in_=st[:, 1, :], func=AF.Ln, bias=epsb[:], scale=1.0),
            reads=[("st", 1), "epsb"], writes=[("st", 1)])
        add("act", lambda e: e.activation(out=st[:, 1, :], in_=st[:, 1, :], func=AF.Exp, scale=-0.5),
            reads=[("st", 1)], writes=[("st", 1)])
        add("dve", lambda e: e.scalar_tensor_tensor(out=st[:, 0, :], in0=ps[PS_SUM][:], scalar=-1.0, in1=st[:, 1, :],
                                                    op0=ALU.mult, op1=ALU.mult),
            reads=[("ps", PS_SUM), ("st", 1)], writes=[("st", 0)])
        gname, bname = ("clg", "clb") if conv else ("slg", "slb")

        def ln_apply(c):
            add("pool", lambda e: e.tensor_tensor(out=cbuf[:, c, :], in0=cbuf[:, c, :], in1=st[:, 1, :], op=ALU.mult),
                reads=[("cbuf", c), ("st", 1)], writes=[("cbuf", c)])
            add("dve", lambda e: e.tensor_tensor(out=cbuf[:, c, :], in0=cbuf[:, c, :], in1=st[:, 0, :], op=ALU.add),
                reads=[("cbuf", c), ("st", 0)], writes=[("cbuf", c)])

        if conv:
            def s2a(c):
                slot = S_win.get(widx[0])
                widx[0] += 1
                bz = psm_new()
                proj_block(bz, slot, 0)
                f3 = fnew_long()
                add("act", lambda e: e.activation(out=fs[:, f3, :], in_=ps[bz][:], func=AF.Silu),
                    reads=[("ps", bz)], writes=[("fs", f3)])
                cst_[c]["f3"] = f3

            def s2b(c):
                ln_apply(c)

            def s2c(c):
                f3 = cst_[c]["f3"]
                gcol, bcol = prm_col(gname, j * EC + c), prm_col(bname, j * EC + c)
                f2 = fnew()
                add("act", lambda e: e.activation(out=fs[:, f2, :], in_=cbuf[:, c, :], func=AF.Silu, bias=bcol, scale=gcol),
                    reads=[("cbuf", c), "prm"], writes=[("fs", f2)])
                add("dve", lambda e: e.tensor_tensor(out=ub[:, c, :], in0=fs[:, f2, :], in1=fs[:, f3, :], op=ALU.mult),
                    reads=[("fs", f2), ("fs", f3)], writes=[("ub", c)])

            pipeline(EC, [s2a, s2b, s2c])
        else:
            def s2a(c):
                slot = S_win.get(widx[0])
                widx[0] += 1
                ba, bz = psm_new(), psm_new()
                proj_block(ba, slot, 0)
                proj_block(bz, slot, 1)
                f2, f3 = fnew_long(), fnew()
                add("act", lambda e: e.activation(out=fs[:, f2, :], in_=ps[ba][:], func=AF.Gelu),
                    reads=[("ps", ba)], writes=[("fs", f2)])
                add("act", lambda e: e.activation(out=fs[:, f3, :], in_=ps[bz][:], func=AF.Silu),
                    reads=[("ps", bz)], writes=[("fs", f3)])
                add("pool", lambda e: e.tensor_tensor(out=fs[:, f2, :], in0=fs[:, f2, :], in1=fs[:, f3, :], op=ALU.mult),
                    reads=[("fs", f2), ("fs", f3)], writes=[("fs", f2)])
                cst_[c]["f2"] = f2

            def sL(c):
                ln_apply(c)

            def sV(c):
                gcol, bcol = prm_col(gname, j * EC + c), prm_col(bname, j * EC + c)
                bv = bnew()
                add("act", lambda e: e.activation(out=bs[:, bv, :], in_=cbuf[:, c, :], func=AF.Identity, bias=bcol, scale=gcol),
                    reads=[("cbuf", c), "prm"], writes=[("bs", bv)])
                cst_[c]["bv"] = bv

            def s2b2(c):
                bv = cst_[c]["bv"]
                bt = psc_new()
                for n in range(4):
                    mm(ps[bt][:, n * 128:(n + 1) * 128], bs[:, bv, n * 128:(n + 1) * 128], ident[:], True, True,
                       [("bs", bv), "ident"], bt)
                bT = bnew()
                add("dve", lambda e: e.tensor_copy(out=bs[:, bT, :], in_=ps[bt][:]),
                    reads=[("ps", bt)], writes=[("bs", bT)])
                cst_[c]["bT"] = bT

            def s2c(c):
                bT, f2 = cst_[c]["bT"], cst_[c]["f2"]
                g = c // 2
                bm = PS_SUM + (c % 2)
                wo = (j * 8 + g) * 128
                for n in range(4):
                    mm(ps[bm][:, n * 128:(n + 1) * 128], bs[:, bT, n * 128:(n + 1) * 128], wtb[:, wo:wo + 128],
                       True, True, [("bs", bT), "wtb"], bm)
                f4 = fnew()
                add("dve", lambda e: e.tensor_tensor(
                    out=fs[:, f4, :].rearrange("p (n t) -> p n t", n=4),
                    in0=ps[bm][:].rearrange("p (n t) -> p n t", n=4),
                    in1=bbc[:, wo:wo + 128].unsqueeze(1).to_broadcast([128, 4, 128]), op=ALU.add),
                    reads=[("ps", bm), "bbc"], writes=[("fs", f4)])
                add("dve", lambda e: e.tensor_tensor(out=ub[:, c, :], in0=fs[:, f4, :], in1=fs[:, f2, :], op=ALU.mult),
                    reads=[("fs", f4), ("fs", f2)], writes=[("ub", c)])

            nop = lambda c: None
            pipeline(EC, [sL, s2a, sV, s2b2, s2c])
        for dc in range(KC):
            s = S_wout.get(widx[1])
            widx[1] += 1
            bo = psm_new()
            for ec in range(EC):
                mm(ps[bo][:], wout[:, s, ec * 128:(ec + 1) * 128], ub[:, ec, :], ec == 0, ec == EC - 1,
                   [("ub", ec), ("wout", s)], bo)
            add("dve", lambda e, dc=dc, bo=bo: e.tensor_tensor(out=xb[:, dc, :], in0=xb[:, dc, :], in1=ps[bo][:], op=ALU.add),
                reads=[("x", dc), ("ps", bo)], writes=[("x", dc)])
            rms_partial(dc)
        S_p.get(widx[3])
        sp_ = S_wp.get(widx[3])
        widx[3] += 1
        add("pool", lambda e: e.tensor_copy(out=pb16[:], in_=pbuf[:]), reads=["pbuf"], writes=["pb16"])
        h_from_x("png", l)
        for dc in range(KC):
            s = S_wg.get(widx[2])
            widx[2] += 1
            bp = psc_new()
            for kc in range(2):
                o = kc * D + dc * 128
                mm(ps[bp][:], wpb[:, sp_, o:o + 128], pb16[:, kc, :], kc == 0, kc == 1, ["pb16", ("wp", sp_)], bp)
            bg = psm_new()
            for kc in range(KC):
                mm(ps[bg][:], wgb[:, s, kc * 128:(kc + 1) * 128], hb[:, kc, :], kc == 0, kc == KC - 1,
                   [("h", kc), ("wg", s)], bg)
            f = fnew()
            add("act", lambda e, f=f, bg=bg: e.activation(out=fs[:, f, :], in_=ps[bg][:], func=AF.Sigmoid),
                reads=[("ps", bg)], writes=[("fs", f)])
            add("dve", lambda e, f=f, bp=bp: e.tensor_tensor(out=fs[:, f, :], in0=fs[:, f, :], in1=ps[bp][:], op=ALU.mult),
                reads=[("fs", f), ("ps", bp)], writes=[("fs", f)])
            add("dve", lambda e, f=f, dc=dc: e.tensor_tensor(out=xb[:, dc, :], in0=xb[:, dc, :], in1=fs[:, f, :], op=ALU.add),
                reads=[("fs", f), ("x", dc)], writes=[("x", dc)])
            rms_partial(dc)

    widx = [0, 0, 0, 0]
    out_keys = []
    for kc in range(KC):
        add("sp", lambda e, kc=kc: e.dma_start(out=xb[:, kc, :], in_=x_d[kc, :, 0:T]),
            writes=[("x", kc)], dma_key=("xl", kc))
        rms_partial(kc)
    for ti in range(ntiles):
        for l in range(depth):
            layer(ti, l, widx, l == depth - 1)
        rms_finalize()
        for kc in range(KC):
            def fin(kc=kc, ti=ti):
                add("dve", lambda e: e.scalar_tensor_tensor(
                    out=cbuf[:, kc, :], in0=xb[:, kc, :], scalar=prm_col("fg", kc), in1=st[:, 2, :],
                    op0=ALU.mult, op1=ALU.mult),
                    reads=[("x", kc), ("st", 2), "prm"], writes=[("cbuf", kc)])
                add("act", lambda e: e.dma_start(out=out_d[kc, :, ti * T:(ti + 1) * T], in_=cbuf[:, kc, :]),
                    reads=[("cbuf", kc)], writes=[("out", ti, kc)], dma_key=("ost", kc))
                if ("ost", kc) not in out_keys:
                    out_keys.append(("ost", kc))
                if ti + 1 < ntiles:
                    add("sp", lambda e: e.dma_start(out=xb[:, kc, :], in_=x_d[kc, :, (ti + 1) * T:(ti + 2) * T]),
                        writes=[("x", kc)], dma_key=("xl", kc))
                    rms_partial(kc)
            fin()

    sch.emit(nc, out_keys)
    es.close()
    return nc


_NC_CACHE = {}


def kernel(x, p, norm_g, w_in, w_out, conv_w, conv_b, conv_ln_g, conv_ln_b,
           sgu_ln_g, sgu_ln_b, sgu_w, sgu_b, pl_norm_g, pl_gate_w, pl_proj_w, final_g):
    x = np.asarray(x, np.float32)
    p = np.asarray(p, np.float32)
    B, S, _ = x.shape
    bpc = B // NCORES
    ntok = bpc * S
    ntiles = ntok // T
    wts = _host_weights(np.asarray(w_in, np.float32), np.asarray(w_out, np.float32),
                        np.asarray(pl_gate_w, np.float32), np.asarray(pl_proj_w, np.float32),
                        np.asarray(sgu_w, np.float32)).reshape(NW // PP, 128, 2048)
    prm = _host_params(*[np.asarray(a, np.float32) for a in
                         (norm_g, pl_norm_g, final_g, conv_w, conv_b, conv_ln_g, conv_ln_b, sgu_ln_g, sgu_ln_b)])
    sgb = np.ascontiguousarray(np.asarray(sgu_b, np.float32).reshape(1, -1))
    in_maps = []
    for c in range(NCORES):
        xc = x[c * bpc:(c + 1) * bpc].reshape(ntok, D)
        x_t = np.ascontiguousarray(xc.T).reshape(KC, 128, ntok)
        pc = p[:, c * bpc:(c + 1) * bpc].reshape(DEPTH, ntok, PLE)
        p_t = np.ascontiguousarray(pc.transpose(0, 2, 1)).reshape(DEPTH, 2, 128, ntok)
        in_maps.append({"x_t": x_t, "p_t": p_t, "wts": wts, "prm": prm, "sgb": sgb})
    nc = build_program(ntiles, DEPTH, S // T)
    res = run_bass_kernel_spmd(nc, in_maps, core_ids=list(range(NCORES)))
    out = np.empty((B, S, D), np.float32)
    for c in range(NCORES):
        o = np.asarray(res.results[c]["out_t"]).reshape(D, ntok)
        out[c * bpc:(c + 1) * bpc] = o.T.reshape(bpc, S, D)
    return out
```

```python
import numpy as np
import concourse.bass as bass
import concourse.mybir as mybir
from concourse.bass_utils import run_bass_kernel_spmd

F32 = mybir.dt.float32
BF16 = mybir.dt.bfloat16
AF = mybir.ActivationFunctionType
ALU = mybir.AluOpType

D = 1024
E = 2048
DEPTH = 4
KW = 31
PLE = 256
NCORES = 8
SEQ = 4096
T = 512
EPS = 1e-6
KC = D // 128
EC = E // 128
HALO = KW - 1
KD = 0
DTAPS = [2 * i for i in range(KD)]
PTAPS = [k for k in range(KW) if k not in DTAPS]

BLK = 128 * 1024
PP = 128 * 2048


def _win_pieces(l):
    if l % 2 == 0:
        p1 = [[(0, c), (1, c)] for c in range(EC)]
        p2 = [[(2, c)] for c in range(EC)]
    else:
        p1 = [[(1, c)] for c in range(EC)]
        p2 = [[(0, c), (2, c)] for c in range(EC)]
    return p1, p2


def _weight_offsets():
    off = 0
    win = {}
    for l in range(DEPTH):
        p1, p2 = _win_pieces(l)
        for ph, pcs in ((1, p1), (2, p2)):
            for c, blks in enumerate(pcs):
                win[(l, ph, c)] = (off, len(blks))
                off += BLK * len(blks)
    wout = {}
    for l in range(DEPTH):
        for dc in range(KC):
            wout[(l, dc)] = off
            off += 128 * EC * 128
    wg = {}
    for l in range(DEPTH):
        for dc in range(KC):
            wg[(l, dc)] = off
            off += 128 * KC * 128
    wp = {}
    for l in range(DEPTH):
        wp[l] = off
        off += 128 * 2 * D
    wsgu = off
    off += 128 * 2 * 8 * 128
    assert off % PP == 0
    return win, wout, wg, wp, wsgu, off


WIN_OFF, WOUT_OFF, WG_OFF, WP_OFF, WSGU_OFF, NW = _weight_offsets()

PRM = {}
_o = 0
for _n, _sz in (("ng", DEPTH * KC), ("png", DEPTH * KC), ("fg", KC), ("cw", 2 * EC * KW),
                ("cb", 2 * EC), ("clg", 2 * EC), ("clb", 2 * EC), ("slg", 2 * EC), ("slb", 2 * EC)):
    PRM[_n] = _o
    _o += _sz
NPRM = _o


def _host_weights(w_in, w_out, pl_gate_w, pl_proj_w, sgu_w):
    parts = []
    for l in range(DEPTH):
        wl = w_in[l].reshape(KC, 128, 3, EC, 128)
        p1, p2 = _win_pieces(l)
        for pcs in (p1, p2):
            for blks in pcs:
                bl = [wl[:, :, j, c, :].transpose(1, 0, 2).reshape(128, 1024) for (j, c) in blks]
                parts.append(np.ascontiguousarray(np.stack(bl, axis=1)).reshape(-1))
    for l in range(DEPTH):
        wl = w_out[l].reshape(EC, 128, KC, 128)
        parts.append(np.ascontiguousarray(wl.transpose(2, 1, 0, 3)).reshape(-1))
    for l in range(DEPTH):
        wl = pl_gate_w[l].reshape(KC, 128, KC, 128)
        parts.append(np.ascontiguousarray(wl.transpose(2, 1, 0, 3)).reshape(-1))
    for l in range(DEPTH):
        wl = pl_proj_w[l].reshape(2, 128, D)
        parts.append(np.ascontiguousarray(wl.transpose(1, 0, 2)).reshape(-1))
    parts.append(np.ascontiguousarray(sgu_w.transpose(3, 0, 1, 2)).reshape(-1))
    flat = np.concatenate(parts).astype(np.float32, copy=False)
    assert flat.size == NW
    return flat


def _host_params(norm_g, pl_norm_g, final_g, conv_w, conv_b, conv_ln_g, conv_ln_b, sgu_ln_g, sgu_ln_b):
    prm = np.zeros((128, NPRM), np.float32)

    def put(name, arr):
        prm[:, PRM[name]:PRM[name] + arr.shape[1]] = arr

    put("ng", norm_g.reshape(DEPTH, KC, 128).transpose(2, 0, 1).reshape(128, -1))
    put("png", pl_norm_g.reshape(DEPTH, KC, 128).transpose(2, 0, 1).reshape(128, -1))
    put("fg", final_g.reshape(KC, 128).T)
    put("cw", conv_w.reshape(2, KW, EC, 128).transpose(3, 0, 2, 1).reshape(128, -1))
    for nm, a in (("cb", conv_b), ("clg", conv_ln_g), ("clb", conv_ln_b), ("slg", sgu_ln_g), ("slb", sgu_ln_b)):
        put(nm, a.reshape(2, EC, 128).transpose(2, 0, 1).reshape(128, -1))
    return prm


class Op:
    __slots__ = ("eng", "fn", "deps", "signal", "is_dma", "dma_key", "sig")

    def __init__(self, eng, fn, is_dma, dma_key):
        self.eng = eng
        self.fn = fn
        self.deps = []
        self.signal = is_dma
        self.is_dma = is_dma
        self.dma_key = dma_key
        self.sig = None


class Sched:
    ENGS = ("pe", "act", "dve", "pool", "sp")

    def __init__(self):
        self.ops = {e: [] for e in self.ENGS}
        self.last_w = {}
        self.readers = {}
        self.frozen = set()
        self.dma_keys = []

    def add(self, eng, fn, reads=(), writes=(), dma_key=None):
        is_dma = dma_key is not None
        op = Op(eng, fn, is_dma, dma_key)
        if is_dma and dma_key not in self.dma_keys:
            self.dma_keys.append(dma_key)
        deps = {}
        for k in reads:
            w = self.last_w.get(k)
            if w is not None:
                deps[id(w)] = w
        for k in writes:
            assert k not in self.frozen, k
            w = self.last_w.get(k)
            if w is not None:
                deps[id(w)] = w
            for r in self.readers.get(k, ()):
                deps[id(r)] = r
        for d in deps.values():
            if d.eng == "pe" and eng == "pe" and not d.is_dma and not is_dma:
                continue
            op.deps.append(d)
            d.signal = True
        for k in reads:
            if k not in self.frozen:
                self.readers.setdefault(k, []).append(op)
        for k in writes:
            self.last_w[k] = op
            self.readers[k] = []
        self.ops[eng].append(op)
        return op

    def freeze(self, *keys):
        self.frozen.update(keys)

    def emit(self, nc, final_wait_keys):
        import contextlib
        with contextlib.ExitStack() as es:
            esem = {e: es.enter_context(nc.semaphore("s_" + e)) for e in self.ENGS}
            dsem = {k: es.enter_context(nc.semaphore("d_" + "".join(ch for ch in str(k) if ch.isalnum()))) for k in self.dma_keys}
            cnt = {e: 0 for e in self.ENGS}
            dcnt = {k: 0 for k in self.dma_keys}
            for e in self.ENGS:
                for op in self.ops[e]:
                    if op.is_dma:
                        dcnt[op.dma_key] += 16
                        op.sig = (dsem[op.dma_key], dcnt[op.dma_key])
                    elif op.signal:
                        cnt[e] += 1
                        op.sig = (esem[e], cnt[e])
            block = es.enter_context(nc.Block())

            def run(engname, engobj, final=False):
                waited = {}
                for op in self.ops[engname]:
                    need = {}
                    for d in op.deps:
                        s, v = d.sig
                        if waited.get(id(s), 0) >= v:
                            continue
                        if need.get(id(s), (None, 0))[1] < v:
                            need[id(s)] = (s, v)
                    for s, v in need.values():
                        engobj.wait_ge(s, v)
                        waited[id(s)] = v
                    ins = op.fn(engobj)
                    if op.sig is not None:
                        ins.then_inc(op.sig[0], 16 if op.is_dma else 1)
                if final:
                    for k in final_wait_keys:
                        engobj.wait_ge(dsem[k], dcnt[k])

            @block.tensor
            def _(e):
                run("pe", e)

            @block.scalar
            def _(e):
                run("act", e)

            @block.vector
            def _(e):
                run("dve", e)

            @block.gpsimd
            def _(e):
                run("pool", e)

            @block.sync
            def _(e):
                run("sp", e, final=True)


class Stream:
    def __init__(self, sch, name, nslots, pieces, issue):
        self.sch, self.name, self.n, self.pieces, self.issue = sch, name, nslots, pieces, issue
        self.next = 0

    def get(self, i):
        hi = min(len(self.pieces), i + self.n)
        while self.next < hi:
            j = self.next
            self.issue(j, j % self.n, self.pieces[j])
            self.next += 1
        return i % self.n


def build_program(ntiles, depth=DEPTH, tiles_per_seq=SEQ // T, do_prepass=True):
    ntok = ntiles * T
    nc = bass.Bass("TRN2", target_bir_lowering=False)
    x_d = nc.dram_tensor("x_t", [KC, 128, ntok], F32, kind="ExternalInput").ap()
    p_d = nc.dram_tensor("p_t", [DEPTH, 2, 128, ntok], F32, kind="ExternalInput").ap()
    w_d = nc.dram_tensor("wts", [NW // PP, 128, 2048], F32, kind="ExternalInput").ap()
    prm_d = nc.dram_tensor("prm", [128, NPRM], F32, kind="ExternalInput").ap()
    sgb_d = nc.dram_tensor("sgb", [1, 2 * 8 * 128], F32, kind="ExternalInput").ap()
    out_d = nc.dram_tensor("out_t", [KC, 128, ntok], F32, kind="ExternalOutput").ap()
    wsc = nc.dram_tensor("wsc", [NW], BF16, kind="Internal").ap()
    wsc_pp = wsc.rearrange("(n p f) -> n p f", p=128, f=2048)

    import contextlib
    es = contextlib.ExitStack()

    def sb(name, shape, dt):
        return es.enter_context(nc.sbuf_tensor(name, shape, dt))

    NWIN = 4
    NF = 12
    xb = sb("xb", [128, KC, T], F32)
    hb = sb("hb", [128, KC, T], BF16)
    cbuf = sb("cbuf", [128, EC, T], F32)
    ub = sb("ub", [128, EC, T], BF16)
    ybuf = sb("ybuf", [128, 2, T + HALO], BF16)
    halo = sb("halo", [128, 2, EC, HALO], BF16)
    diag = sb("diag", [128, 2, KW, 128], BF16)
    fs = sb("fs", [128, NF, T], F32)
    bs = sb("bs", [128, 8, T], BF16)
    st = sb("st", [128, 4, T], F32)
    pbuf = sb("pbuf", [128, 2, T], F32)
    pb16 = sb("pb16", [128, 2, T], BF16)
    win = sb("win", [128, NWIN, 2048], BF16)
    wout = sb("wout", [128, 2, EC * 128], BF16)
    wgb = sb("wgb", [128, 2, KC * 128], BF16)
    wpb = sb("wpb", [128, 2, 2 * D], BF16)
    wtb = sb("wtb", [128, 2 * 8 * 128], BF16)
    bbc = sb("bbc", [128, 2 * 8 * 128], F32)
    prm = sb("prm_sb", [128, NPRM], F32)
    ident = sb("ident", [128, 128], BF16)
    onesD = sb("onesD", [128, 128], BF16)
    onesE = sb("onesE", [128, 128], BF16)
    onesf = sb("onesf", [128, 128], F32)
    epsb = sb("epsb", [128, 1], F32)
    ps = [es.enter_context(nc.psum_tensor("ps%d" % i, [128, T], F32)) for i in range(8)]

    sch = Sched()
    add = sch.add

    def prm_col(name, idx):
        o = PRM[name] + idx
        return prm[:, o:o + 1]

    add("sp", lambda e: e.dma_start(out=prm[:], in_=prm_d), writes=["prm"], dma_key="setup")
    add("sp", lambda e: e.dma_start(out=bbc[:], in_=sgb_d.partition_broadcast(128)), writes=["bbc"], dma_key="setup2")
    add("pool", lambda e: e.memset(onesf[:], 1.0), writes=["onesf"])
    add("pool", lambda e: e.memset(onesD[:], 1.0 / D), writes=["onesD"])
    add("pool", lambda e: e.memset(epsb[:], EPS), writes=["epsb"])
    add("pool", lambda e: e.memset(onesE[:], 1.0 / E), writes=["onesE"])
    add("pool", lambda e: e.affine_select(out=ident[:], in_=onesf[:], pattern=[[-1, 128]],
                                          compare_op=ALU.is_equal, fill=0.0, base=0, channel_multiplier=1),
        reads=["onesf"], writes=["ident"])
    sch.freeze("prm", "bbc", "onesD", "onesE", "ident", "onesf", "epsb")

    cst = cbuf[:].rearrange("p (s c) t -> p s (c t)", s=4)
    ust = ub[:].rearrange("p (s c) t -> p s (c t)", s=4)
    npp = NW // PP
    if do_prepass:
        cast_engs = ("act", "dve")
        for i in range(npp):
            s = i % 4
            ckeys = [("cbuf", 4 * s + q) for q in range(4)]
            ukeys = [("ub", 4 * s + q) for q in range(4)]
            add("sp", lambda e, i=i, s=s: e.dma_start(out=cst[:, s, :], in_=w_d[i]),
                writes=ckeys, dma_key=("ppl", s))
            ce = cast_engs[i % 2]
            if ce == "act":
                add("act", lambda e, s=s: e.copy(out=ust[:, s, :], in_=cst[:, s, :]), reads=ckeys, writes=ukeys)
            else:
                add(ce, lambda e, s=s: e.tensor_copy(out=ust[:, s, :], in_=cst[:, s, :]), reads=ckeys, writes=ukeys)
            add("sp", lambda e, i=i, s=s: e.dma_start(out=wsc_pp[i], in_=ust[:, s, :]),
                reads=ukeys, writes=[("wsc", i)], dma_key=("pps", s))
    all_wsc = [("wsc", i) for i in range(npp)]

    def wsc_view(off, n):
        return wsc[off:off + n].rearrange("(p f) -> p f", p=128)

    add("sp", lambda e: e.dma_start(out=wtb[:], in_=wsc_view(WSGU_OFF, 128 * 2048)),
        reads=all_wsc, writes=["wtb"], dma_key="setup3")
    wtb3 = wtb[:].rearrange("p (a t) -> p a t", t=128)
    add("pool", lambda e: e.affine_select(out=wtb3, in_=wtb3, pattern=[[0, 16], [1, 128]],
                                          compare_op=ALU.is_ge, fill=0.0, base=0, channel_multiplier=-1),
        reads=["wtb"], writes=["wtb"])
    sch.freeze("wtb")

    win_pieces, wout_pieces, wg_pieces, wp_pieces, p_pieces = [], [], [], [], []
    for ti in range(ntiles):
        for l in range(depth):
            for ph in (1, 2):
                for c in range(EC):
                    win_pieces.append(WIN_OFF[(l, ph, c)])
            for dc in range(KC):
                wout_pieces.append(WOUT_OFF[(l, dc)])
                wg_pieces.append(WG_OFF[(l, dc)])
            wp_pieces.append(WP_OFF[l])
            p_pieces.append((ti, l))

    def issue_win(j, s, pc):
        off, nb = pc
        add("sp", lambda e: e.dma_start(out=win[:, s, 0:nb * 1024], in_=wsc_view(off, BLK * nb)),
            reads=all_wsc if j < NWIN else (), writes=[("win", s)], dma_key=("win", s))

    def issue_wout(j, s, off):
        add("sp", lambda e: e.dma_start(out=wout[:, s, :], in_=wsc_view(off, 128 * 2048)),
            reads=all_wsc if j < 2 else (), writes=[("wout", s)], dma_key=("wout", s))

    def issue_wg(j, s, off):
        add("sp", lambda e: e.dma_start(out=wgb[:, s, :], in_=wsc_view(off, 128 * 1024)),
            reads=all_wsc if j < 2 else (), writes=[("wg", s)], dma_key=("wg", s))

    def issue_wp(j, s, off):
        add("sp", lambda e: e.dma_start(out=wpb[:, s, :], in_=wsc_view(off, 128 * 2048)),
            reads=all_wsc if j < 2 else (), writes=[("wp", s)], dma_key=("wp", s))

    def issue_p(j, s, pc):
        ti, l = pc
        add("sp", lambda e: e.dma_start(out=pbuf[:],
                                        in_=p_d[l, :, :, ti * T:(ti + 1) * T].rearrange("k p t -> p k t")),
            writes=["pbuf"], dma_key="pld")

    S_win = Stream(sch, "win", NWIN, win_pieces, issue_win)
    S_wout = Stream(sch, "wout", 2, wout_pieces, issue_wout)
    S_wg = Stream(sch, "wg", 2, wg_pieces, issue_wg)
    S_wp = Stream(sch, "wp", 2, wp_pieces, issue_wp)
    S_p = Stream(sch, "p", 1, p_pieces, issue_p)

    rot = {"f": 0, "fl": 0, "b": 0, "y": 0, "dg": 0, "psm": 0, "psc": 0}

    NFL = 6

    def fnew():
        i = NFL + rot["f"] % (NF - NFL)
        rot["f"] += 1
        return i

    def fnew_long():
        i = rot["fl"] % NFL
        rot["fl"] += 1
        return i

    def bnew():
        i = rot["b"] % 8
        rot["b"] += 1
        return i

    def psm_new():
        i = rot["psm"] % 4
        rot["psm"] += 1
        return i

    def psc_new():
        i = 4 + rot["psc"] % 2
        rot["psc"] += 1
        return i

    PS_SUM, PS_SQ = 6, 7

    def mm(out, lhsT, rhs, start, stop, reads, bank):
        add("pe", lambda e: e.matmul(out, lhsT, rhs, start=start, stop=stop), reads=reads, writes=[("ps", bank)])

    rms_state = {"n": 0, "pend": None}

    def rms_flush():
        pd = rms_state["pend"]
        if pd is not None:
            b, first, last = pd
            mm(ps[PS_SUM][:], onesD[:], bs[:, b, :], first, last, [("bs", b), "onesD"], PS_SUM)
            rms_state["pend"] = None

    def rms_partial(kc):
        rms_flush()
        b = bnew()
        add("act", lambda e: e.activation(out=bs[:, b, :], in_=xb[:, kc, :], func=AF.Square),
            reads=[("x", kc)], writes=[("bs", b)])
        n = rms_state["n"]
        rms_state["pend"] = (b, n == 0, n == KC - 1)
        rms_state["n"] = (n + 1) % KC

    def rms_finalize():
        rms_flush()
        assert rms_state["n"] == 0
        add("act", lambda e: e.activation(out=st[:, 2, :], in_=ps[PS_SUM][:], func=AF.Ln, bias=epsb[:], scale=1.0),
            reads=[("ps", PS_SUM), "epsb"], writes=[("st", 2)])
        add("act", lambda e: e.activation(out=st[:, 2, :], in_=st[:, 2, :], func=AF.Exp, scale=-0.5),
            reads=[("st", 2)], writes=[("st", 2)])

    def h_from_x(gname, l):
        rms_finalize()
        for kc in range(KC):
            eng = "dve"
            add(eng, lambda e, kc=kc: e.scalar_tensor_tensor(
                out=hb[:, kc, :], in0=xb[:, kc, :], scalar=prm_col(gname, l * KC + kc), in1=st[:, 2, :],
                op0=ALU.mult, op1=ALU.mult),
                reads=[("x", kc), ("st", 2), "prm"], writes=[("h", kc)])

    def proj_block(bank, slot, blk):
        for kc in range(KC):
            o = blk * 1024 + kc * 128
            mm(ps[bank][:], win[:, slot, o:o + 128], hb[:, kc, :], kc == 0, kc == KC - 1,
               [("h", kc), ("win", slot)], bank)

    def pipeline(n, stages):
        ns = len(stages)
        for i in range(n + ns - 1):
            for k, stg in enumerate(stages):
                c = i - k
                if 0 <= c < n:
                    stg(c)

    def layer(ti, l, widx, last_layer):
        j = l // 2
        conv = (l % 2 == 0)
        seq_start = (ti % tiles_per_seq == 0)
        h_from_x("ng", l)
        cst_ = [dict() for _ in range(EC)]

        def stats_prep(c):
            b1, b2 = bnew(), bnew()
            add("act", lambda e: e.activation(out=bs[:, b1, :], in_=cbuf[:, c, :], func=AF.Square),
                reads=[("cbuf", c)], writes=[("bs", b1)])
            add("dve", lambda e: e.tensor_copy(out=bs[:, b2, :], in_=cbuf[:, c, :]),
                reads=[("cbuf", c)], writes=[("bs", b2)])
            cst_[c]["b1"], cst_[c]["b2"] = b1, b2

        def stats_mm(c):
            b1, b2 = cst_[c]["b1"], cst_[c]["b2"]
            mm(ps[PS_SUM][:], onesE[:], bs[:, b2, :], c == 0, c == EC - 1, [("bs", b2), "onesE"], PS_SUM)
            mm(ps[PS_SQ][:], onesE[:], bs[:, b1, :], c == 0, c == EC - 1, [("bs", b1), "onesE"], PS_SQ)

        if conv:
            def sA(c):
                slot = S_win.get(widx[0])
                widx[0] += 1
                ba, bb_ = psm_new(), psm_new()
                proj_block(ba, slot, 0)
                proj_block(bb_, slot, 1)
                f = fnew()
                add("act", lambda e: e.activation(out=fs[:, f, :], in_=ps[bb_][:], func=AF.Sigmoid),
                    reads=[("ps", bb_)], writes=[("fs", f)])
                y = rot["y"] % 2
                rot["y"] += 1
                if seq_start:
                    add("pool", lambda e: e.memset(ybuf[:, y, 0:HALO], 0.0), writes=[("yh", y)])
                else:
                    add("pool", lambda e: e.tensor_copy(out=ybuf[:, y, 0:HALO], in_=halo[:, j, c, :]),
                        reads=[("halo", j, c)], writes=[("yh", y)])
                add("dve", lambda e: e.tensor_tensor(out=ybuf[:, y, HALO:HALO + T], in0=ps[ba][:],
                                                     in1=fs[:, f, :], op=ALU.mult),
                    reads=[("ps", ba), ("fs", f)], writes=[("y", y)])
                add("pool", lambda e: e.tensor_copy(out=halo[:, j, c, :], in_=ybuf[:, y, T:T + HALO]),
                    reads=[("y", y)], writes=[("halo", j, c)])
                dg = rot["dg"] % 2
                rot["dg"] += 1
                cwo = PRM["cw"] + (j * EC + c) * KW
                add("pool", lambda e: e.affine_select(
                    out=diag[:, dg, :, :], in_=prm[:, cwo:cwo + KW].unsqueeze(2).to_broadcast([128, KW, 128]),
                    pattern=[[0, KW], [-1, 128]], compare_op=ALU.is_equal, fill=0.0, base=0, channel_multiplier=1),
                    reads=["prm"], writes=[("diag", dg)])
                cst_[c]["y"], cst_[c]["dg"] = y, dg

            def sB(c):
                y, dg = cst_[c]["y"], cst_[c]["dg"]
                bc = psc_new()
                for k in PTAPS:
                    mm(ps[bc][:], diag[:, dg, k, :], ybuf[:, y, k:k + T], k == PTAPS[0], k == PTAPS[-1],
                       [("diag", dg), ("y", y), ("yh", y)], bc)
                cwo = PRM["cw"] + (j * EC + c) * KW
                if KD > 0:
                    fa = fnew()
                    k0 = DTAPS[0]
                    add("dve", lambda e: e.tensor_scalar(out=fs[:, fa, :], in0=ybuf[:, y, k0:k0 + T], scalar1=prm[:, cwo + k0:cwo + k0 + 1],
                                                         scalar2=None, op0=ALU.mult),
                        reads=[("y", y), ("yh", y), "prm"], writes=[("fs", fa)])
                    for k in DTAPS[1:]:
                        add("dve", lambda e, k=k: e.scalar_tensor_tensor(
                            out=fs[:, fa, :], in0=ybuf[:, y, k:k + T], scalar=prm[:, cwo + k:cwo + k + 1], in1=fs[:, fa, :],
                            op0=ALU.mult, op1=ALU.add),
                            reads=[("y", y), ("yh", y), "prm", ("fs", fa)], writes=[("fs", fa)])
                    add("dve", lambda e: e.scalar_tensor_tensor(
                        out=cbuf[:, c, :], in0=ps[bc][:], scalar=prm_col("cb", j * EC + c), in1=fs[:, fa, :],
                        op0=ALU.add, op1=ALU.add),
                        reads=[("ps", bc), "prm", ("fs", fa)], writes=[("cbuf", c)])
                else:
                    add("act", lambda e: e.activation(out=cbuf[:, c, :], in_=ps[bc][:], func=AF.Identity,
                                                      bias=prm_col("cb", j * EC + c), scale=1.0),
                        reads=[("ps", bc), "prm"], writes=[("cbuf", c)])
                stats_prep(c)

            pipeline(EC, [sA, sB, stats_mm])
        else:
            def sA(c):
                slot = S_win.get(widx[0])
                widx[0] += 1
                bb_ = psm_new()
                proj_block(bb_, slot, 0)
                add("act", lambda e: e.activation(out=cbuf[:, c, :], in_=ps[bb_][:], func=AF.Gelu),
                    reads=[("ps", bb_)], writes=[("cbuf", c)])
                stats_prep(c)

            pipeline(EC, [sA, stats_mm])
        add("act", lambda e: e.activation(out=st[:, 3, :], in_=ps[PS_SUM][:], func=AF.Square),
            reads=[("ps", PS_SUM)], writes=[("st", 3)])
        add("dve", lambda e: e.tensor_tensor(out=st[:, 1, :], in0=ps[PS_SQ][:], in1=st[:, 3, :], op=ALU.subtract),
            reads=[("ps", PS_SQ), ("st", 3)], writes=[("st", 1)])
        add("act", lambda e: e.activation(out=st[:, 1, :], in_=st[:, 1, :], func=AF.Ln, bias=epsb[:], scale=1.0),
            reads=[("st", 1), "epsb"], writes=[("st", 1)])
        add("act", lambda e: e.activation(out=st[:, 1, :], in_=st[:, 1, :], func=AF.Exp, scale=-0.5),
            reads=[("st", 1)], writes=[("st", 1)])
        add("dve", lambda e: e.scalar_tensor_tensor(out=st[:, 0, :], in0=ps[PS_SUM][:], scalar=-1.0, in1=st[:, 1, :],
                                                    op0=ALU.mult, op1=ALU.mult),
            reads=[("ps", PS_SUM), ("st", 1)], writes=[("st", 0)])
        gname, bname = ("clg", "clb") if conv else ("slg", "slb")

        def ln_apply(c):
            add("pool", lambda e: e.tensor_tensor(out=cbuf[:, c, :], in0=cbuf[:, c, :], in1=st[:, 1, :], op=ALU.mult),
                reads=[("cbuf", c), ("st", 1)], writes=[("cbuf", c)])
            add("dve", lambda e: e.tensor_tensor(out=cbuf[:, c, :], in0=cbuf[:, c, :], in1=st[:, 0, :], op=ALU.add),
                reads=[("cbuf", c), ("st", 0)], writes=[("cbuf", c)])

        if conv:
            def s2a(c):
                slot = S_win.get(widx[0])
                widx[0] += 1
                bz = psm_new()
                proj_block(bz, slot, 0)
                f3 = fnew_long()
                add("act", lambda e: e.activation(out=fs[:, f3, :], in_=ps[bz][:], func=AF.Silu),
                    reads=[("ps", bz)], writes=[("fs", f3)])
                cst_[c]["f3"] = f3

            def s2b(c):
                ln_apply(c)

            def s2c(c):
                f3 = cst_[c]["f3"]
                gcol, bcol = prm_col(gname, j * EC + c), prm_col(bname, j * EC + c)
                f2 = fnew()
                add("act", lambda e: e.activation(out=fs[:, f2, :], in_=cbuf[:, c, :], func=AF.Silu, bias=bcol, scale=gcol),
                    reads=[("cbuf", c), "prm"], writes=[("fs", f2)])
                add("dve", lambda e: e.tensor_tensor(out=ub[:, c, :], in0=fs[:, f2, :], in1=fs[:, f3, :], op=ALU.mult),
                    reads=[("fs", f2), ("fs", f3)], writes=[("ub", c)])

            pipeline(EC, [s2a, s2b, s2c])
        else:
            def s2a(c):
                slot = S_win.get(widx[0])
                widx[0] += 1
                ba, bz = psm_new(), psm_new()
                proj_block(ba, slot, 0)
                proj_block(bz, slot, 1)
                f2, f3 = fnew_long(), fnew()
                add("act", lambda e: e.activation(out=fs[:, f2, :], in_=ps[ba][:], func=AF.Gelu),
                    reads=[("ps", ba)], writes=[("fs", f2)])
                add("act", lambda e: e.activation(out=fs[:, f3, :], in_=ps[bz][:], func=AF.Silu),
                    reads=[("ps", bz)], writes=[("fs", f3)])
                add("pool", lambda e: e.tensor_tensor(out=fs[:, f2, :], in0=fs[:, f2, :], in1=fs[:, f3, :], op=ALU.mult),
                    reads=[("fs", f2), ("fs", f3)], writes=[("fs", f2)])
                cst_[c]["f2"] = f2

            def sL(c):
                ln_apply(c)

            def sV(c):
                gcol, bcol = prm_col(gname, j * EC + c), prm_col(bname, j * EC + c)
                bv = bnew()
                add("act", lambda e: e.activation(out=bs[:, bv, :], in_=cbuf[:, c, :], func=AF.Identity, bias=bcol, scale=gcol),
                    reads=[("cbuf", c), "prm"], writes=[("bs", bv)])
                cst_[c]["bv"] = bv

            def s2b2(c):
                bv = cst_[c]["bv"]
                bt = psc_new()
                for n in range(4):
                    mm(ps[bt][:, n * 128:(n + 1) * 128], bs[:, bv, n * 128:(n + 1) * 128], ident[:], True, True,
                       [("bs", bv), "ident"], bt)
                bT = bnew()
                add("dve", lambda e: e.tensor_copy(out=bs[:, bT, :], in_=ps[bt][:]),
                    reads=[("ps", bt)], writes=[("bs", bT)])
                cst_[c]["bT"] = bT

            def s2c(c):
                bT, f2 = cst_[c]["bT"], cst_[c]["f2"]
                g = c // 2
                bm = PS_SUM + (c % 2)
                wo = (j * 8 + g) * 128
                for n in range(4):
                    mm(ps[bm][:, n * 128:(n + 1) * 128], bs[:, bT, n * 128:(n + 1) * 128], wtb[:, wo:wo + 128],
                       True, True, [("bs", bT), "wtb"], bm)
                f4 = fnew()
                add("dve", lambda e: e.tensor_tensor(
                    out=fs[:, f4, :].rearrange("p (n t) -> p n t", n=4),
                    in0=ps[bm][:].rearrange("p (n t) -> p n t", n=4),
                    in1=bbc[:, wo:wo + 128].unsqueeze(1).to_broadcast([128, 4, 128]), op=ALU.add),
                    reads=[("ps", bm), "bbc"], writes=[("fs", f4)])
                add("dve", lambda e: e.tensor_tensor(out=ub[:, c, :], in0=fs[:, f4, :], in1=fs[:, f2, :], op=ALU.mult),
                    reads=[("fs", f4), ("fs", f2)], writes=[("ub", c)])

            nop = lambda c: None
            pipeline(EC, [sL, s2a, sV, s2b2, s2c])
        for dc in range(KC):
            s = S_wout.get(widx[1])
            widx[1] += 1
            bo = psm_new()
            for ec in range(EC):
                mm(ps[bo][:], wout[:, s, ec * 128:(ec + 1) * 128], ub[:, ec, :], ec == 0, ec == EC - 1,
                   [("ub", ec), ("wout", s)], bo)
            add("dve", lambda e, dc=dc, bo=bo: e.tensor_tensor(out=xb[:, dc, :], in0=xb[:, dc, :], in1=ps[bo][:], op=ALU.add),
                reads=[("x", dc), ("ps", bo)], writes=[("x", dc)])
            rms_partial(dc)
        S_p.get(widx[3])
        sp_ = S_wp.get(widx[3])
        widx[3] += 1
        add("pool", lambda e: e.tensor_copy(out=pb16[:], in_=pbuf[:]), reads=["pbuf"], writes=["pb16"])
        h_from_x("png", l)
        for dc in range(KC):
            s = S_wg.get(widx[2])
            widx[2] += 1
            bp = psc_new()
            for kc in range(2):
                o = kc * D + dc * 128
                mm(ps[bp][:], wpb[:, sp_, o:o + 128], pb16[:, kc, :], kc == 0, kc == 1, ["pb16", ("wp", sp_)], bp)
            bg = psm_new()
            for kc in range(KC):
                mm(ps[bg][:], wgb[:, s, kc * 128:(kc + 1) * 128], hb[:, kc, :], kc == 0, kc == KC - 1,
                   [("h", kc), ("wg", s)], bg)
            f = fnew()
            add("act", lambda e, f=f, bg=bg: e.activation(out=fs[:, f, :], in_=ps[bg][:], func=AF.Sigmoid),
                reads=[("ps", bg)], writes=[("fs", f)])
            add("dve", lambda e, f=f, bp=bp: e.tensor_tensor(out=fs[:, f, :], in0=fs[:, f, :], in1=ps[bp][:], op=ALU.mult),
                reads=[("fs", f), ("ps", bp)], writes=[("fs", f)])
            add("dve", lambda e, f=f, dc=dc: e.tensor_tensor(out=xb[:, dc, :], in0=xb[:, dc, :], in1=fs[:, f, :], op=ALU.add),
                reads=[("fs", f), ("x", dc)], writes=[("x", dc)])
            rms_partial(dc)

    widx = [0, 0, 0, 0]
    out_keys = []
    for kc in range(KC):
        add("sp", lambda e, kc=kc: e.dma_start(out=xb[:, kc, :], in_=x_d[kc, :, 0:T]),
            writes=[("x", kc)], dma_key=("xl", kc))
        rms_partial(kc)
    for ti in range(ntiles):
        for l in range(depth):
            layer(ti, l, widx, l == depth - 1)
        rms_finalize()
        for kc in range(KC):
            def fin(kc=kc, ti=ti):
                add("dve", lambda e: e.scalar_tensor_tensor(
                    out=cbuf[:, kc, :], in0=xb[:, kc, :], scalar=prm_col("fg", kc), in1=st[:, 2, :],
                    op0=ALU.mult, op1=ALU.mult),
                    reads=[("x", kc), ("st", 2), "prm"], writes=[("cbuf", kc)])
                add("act", lambda e: e.dma_start(out=out_d[kc, :, ti * T:(ti + 1) * T], in_=cbuf[:, kc, :]),
                    reads=[("cbuf", kc)], writes=[("out", ti, kc)], dma_key=("ost", kc))
                if ("ost", kc) not in out_keys:
                    out_keys.append(("ost", kc))
                if ti + 1 < ntiles:
                    add("sp", lambda e: e.dma_start(out=xb[:, kc, :], in_=x_d[kc, :, (ti + 1) * T:(ti + 2) * T]),
                        writes=[("x", kc)], dma_key=("xl", kc))
                    rms_partial(kc)
            fin()

    sch.emit(nc, out_keys)
    es.close()
    return nc


_NC_CACHE = {}


def kernel(x, p, norm_g, w_in, w_out, conv_w, conv_b, conv_ln_g, conv_ln_b,
           sgu_ln_g, sgu_ln_b, sgu_w, sgu_b, pl_norm_g, pl_gate_w, pl_proj_w, final_g):
    x = np.asarray(x, np.float32)
    p = np.asarray(p, np.float32)
    B, S, _ = x.shape
    bpc = B // NCORES
    ntok = bpc * S
    ntiles = ntok // T
    wts = _host_weights(np.asarray(w_in, np.float32), np.asarray(w_out, np.float32),
                        np.asarray(pl_gate_w, np.float32), np.asarray(pl_proj_w, np.float32),
                        np.asarray(sgu_w, np.float32)).reshape(NW // PP, 128, 2048)
    prm = _host_params(*[np.asarray(a, np.float32) for a in
                         (norm_g, pl_norm_g, final_g, conv_w, conv_b, conv_ln_g, conv_ln_b, sgu_ln_g, sgu_ln_b)])
    sgb = np.ascontiguousarray(np.asarray(sgu_b, np.float32).reshape(1, -1))
    in_maps = []
    for c in range(NCORES):
        xc = x[c * bpc:(c + 1) * bpc].reshape(ntok, D)
        x_t = np.ascontiguousarray(xc.T).reshape(KC, 128, ntok)
        pc = p[:, c * bpc:(c + 1) * bpc].reshape(DEPTH, ntok, PLE)
        p_t = np.ascontiguousarray(pc.transpose(0, 2, 1)).reshape(DEPTH, 2, 128, ntok)
        in_maps.append({"x_t": x_t, "p_t": p_t, "wts": wts, "prm": prm, "sgb": sgb})
    nc = build_program(ntiles, DEPTH, S // T)
    res = run_bass_kernel_spmd(nc, in_maps, core_ids=list(range(NCORES)))
    out = np.empty((B, S, D), np.float32)
    for c in range(NCORES):
        o = np.asarray(res.results[c]["out_t"]).reshape(D, ntok)
        out[c * bpc:(c + 1) * bpc] = o.T.reshape(bpc, S, D)
    return out
```

```python
import numpy as np
import concourse.bass as bass
import concourse.mybir as mybir
from concourse.bass_utils import run_bass_kernel_spmd

F32 = mybir.dt.float32
BF16 = mybir.dt.bfloat16
AF = mybir.ActivationFunctionType
ALU = mybir.AluOpType

D = 1024
E = 2048
DEPTH = 4
KW = 31
PLE = 256
NCORES = 8
SEQ = 4096
T = 512
EPS = 1e-6
KC = D // 128
EC = E // 128
HALO = KW - 1
KD = 0
DTAPS = [2 * i for i in range(KD)]
PTAPS = [k for k in range(KW) if k not in DTAPS]

BLK = 128 * 1024
PP = 128 * 2048


def _win_pieces(l):
    if l % 2 == 0:
        p1 = [[(0, c), (1, c)] for c in range(EC)]
        p2 = [[(2, c)] for c in range(EC)]
    else:
        p1 = [[(1, c)] for c in range(EC)]
        p2 = [[(0, c), (2, c)] for c in range(EC)]
    return p1, p2


def _weight_offsets():
    off = 0
    win = {}
    for l in range(DEPTH):
        p1, p2 = _win_pieces(l)
        for ph, pcs in ((1, p1), (2, p2)):
            for c, blks in enumerate(pcs):
                win[(l, ph, c)] = (off, len(blks))
                off += BLK * len(blks)
    wout = {}
    for l in range(DEPTH):
        for dc in range(KC):
            wout[(l, dc)] = off
            off += 128 * EC * 128
    wg = {}
    for l in range(DEPTH):
        for dc in range(KC):
            wg[(l, dc)] = off
            off += 128 * KC * 128
    wp = {}
    for l in range(DEPTH):
        wp[l] = off
        off += 128 * 2 * D
    wsgu = off
    off += 128 * 2 * 8 * 128
    assert off % PP == 0
    return win, wout, wg, wp, wsgu, off


WIN_OFF, WOUT_OFF, WG_OFF, WP_OFF, WSGU_OFF, NW = _weight_offsets()

PRM = {}
_o = 0
for _n, _sz in (("ng", DEPTH * KC), ("png", DEPTH * KC), ("fg", KC), ("cw", 2 * EC * KW),
                ("cb", 2 * EC), ("clg", 2 * EC), ("clb", 2 * EC), ("slg", 2 * EC), ("slb", 2 * EC)):
    PRM[_n] = _o
    _o += _sz
NPRM = _o


def _host_weights(w_in, w_out, pl_gate_w, pl_proj_w, sgu_w):
    parts = []
    for l in range(DEPTH):
        wl = w_in[l].reshape(KC, 128, 3, EC, 128)
        p1, p2 = _win_pieces(l)
        for pcs in (p1, p2):
            for blks in pcs:
                bl = [wl[:, :, j, c, :].transpose(1, 0, 2).reshape(128, 1024) for (j, c) in blks]
                parts.append(np.ascontiguousarray(np.stack(bl, axis=1)).reshape(-1))
    for l in range(DEPTH):
        wl = w_out[l].reshape(EC, 128, KC, 128)
        parts.append(np.ascontiguousarray(wl.transpose(2, 1, 0, 3)).reshape(-1))
    for l in range(DEPTH):
        wl = pl_gate_w[l].reshape(KC, 128, KC, 128)
        parts.append(np.ascontiguousarray(wl.transpose(2, 1, 0, 3)).reshape(-1))
    for l in range(DEPTH):
        wl = pl_proj_w[l].reshape(2, 128, D)
        parts.append(np.ascontiguousarray(wl.transpose(1, 0, 2)).reshape(-1))
    parts.append(np.ascontiguousarray(sgu_w.transpose(3, 0, 1, 2)).reshape(-1))
    flat = np.concatenate(parts).astype(np.float32, copy=False)
    assert flat.size == NW
    return flat


def _host_cw4(conv_w):
    wpad = np.zeros((2, 32, E), np.float32)
    wpad[:, :KW] = conv_w
    a = wpad.reshape(2, 4, 8, EC, 4, 32)
    return np.ascontiguousarray(a.transpose(1, 5, 0, 3, 4, 2)).reshape(128, 2 * EC * 32)


def _host_params(norm_g, pl_norm_g, final_g, conv_w, conv_b, conv_ln_g, conv_ln_b, sgu_ln_g, sgu_ln_b):
    prm = np.zeros((128, NPRM), np.float32)

    def put(name, arr):
        prm[:, PRM[name]:PRM[name] + arr.shape[1]] = arr

    put("ng", norm_g.reshape(DEPTH, KC, 128).transpose(2, 0, 1).reshape(128, -1))
    put("png", pl_norm_g.reshape(DEPTH, KC, 128).transpose(2, 0, 1).reshape(128, -1))
    put("fg", final_g.reshape(KC, 128).T)
    put("cw", conv_w.reshape(2, KW, EC, 128).transpose(3, 0, 2, 1).reshape(128, -1))
    for nm, a in (("cb", conv_b), ("clg", conv_ln_g), ("clb", conv_ln_b), ("slg", sgu_ln_g), ("slb", sgu_ln_b)):
        put(nm, a.reshape(2, EC, 128).transpose(2, 0, 1).reshape(128, -1))
    return prm


class Op:
    __slots__ = ("eng", "fn", "deps", "signal", "is_dma", "dma_key", "sig")

    def __init__(self, eng, fn, is_dma, dma_key):
        self.eng = eng
        self.fn = fn
        self.deps = []
        self.signal = is_dma
        self.is_dma = is_dma
        self.dma_key = dma_key
        self.sig = None


class Sched:
    ENGS = ("pe", "act", "dve", "pool", "sp")

    def __init__(self):
        self.ops = {e: [] for e in self.ENGS}
        self.last_w = {}
        self.readers = {}
        self.frozen = set()
        self.dma_keys = []

    def add(self, eng, fn, reads=(), writes=(), dma_key=None):
        is_dma = dma_key is not None
        op = Op(eng, fn, is_dma, dma_key)
        if is_dma and dma_key not in self.dma_keys:
            self.dma_keys.append(dma_key)
        deps = {}
        for k in reads:
            w = self.last_w.get(k)
            if w is not None:
                deps[id(w)] = w
        for k in writes:
            assert k not in self.frozen, k
            w = self.last_w.get(k)
            if w is not None:
                deps[id(w)] = w
            for r in self.readers.get(k, ()):
                deps[id(r)] = r
        for d in deps.values():
            if d.eng == "pe" and eng == "pe" and not d.is_dma and not is_dma:
                continue
            op.deps.append(d)
            d.signal = True
        for k in reads:
            if k not in self.frozen:
                self.readers.setdefault(k, []).append(op)
        for k in writes:
            self.last_w[k] = op
            self.readers[k] = []
        self.ops[eng].append(op)
        return op

    def freeze(self, *keys):
        self.frozen.update(keys)

    def emit(self, nc, final_wait_keys):
        import contextlib
        with contextlib.ExitStack() as es:
            esem = {e: es.enter_context(nc.semaphore("s_" + e)) for e in self.ENGS}
            dsem = {k: es.enter_context(nc.semaphore("d_" + "".join(ch for ch in str(k) if ch.isalnum()))) for k in self.dma_keys}
            cnt = {e: 0 for e in self.ENGS}
            dcnt = {k: 0 for k in self.dma_keys}
            for e in self.ENGS:
                for op in self.ops[e]:
                    if op.is_dma:
                        dcnt[op.dma_key] += 16
                        op.sig = (dsem[op.dma_key], dcnt[op.dma_key])
                    elif op.signal:
                        cnt[e] += 1
                        op.sig = (esem[e], cnt[e])
            block = es.enter_context(nc.Block())

            def run(engname, engobj, final=False):
                waited = {}
                for op in self.ops[engname]:
                    need = {}
                    for d in op.deps:
                        s, v = d.sig
                        if waited.get(id(s), 0) >= v:
                            continue
                        if need.get(id(s), (None, 0))[1] < v:
                            need[id(s)] = (s, v)
                    for s, v in need.values():
                        engobj.wait_ge(s, v)
                        waited[id(s)] = v
                    ins = op.fn(engobj)
                    if op.sig is not None:
                        ins.then_inc(op.sig[0], 16 if op.is_dma else 1)
                if final:
                    for k in final_wait_keys:
                        engobj.wait_ge(dsem[k], dcnt[k])

            @block.tensor
            def _(e):
                run("pe", e)

            @block.scalar
            def _(e):
                run("act", e)

            @block.vector
            def _(e):
                run("dve", e)

            @block.gpsimd
            def _(e):
                run("pool", e)

            @block.sync
            def _(e):
                run("sp", e, final=True)


class Stream:
    def __init__(self, sch, name, nslots, pieces, issue):
        self.sch, self.name, self.n, self.pieces, self.issue = sch, name, nslots, pieces, issue
        self.next = 0

    def get(self, i):
        hi = min(len(self.pieces), i + self.n)
        while self.next < hi:
            j = self.next
            self.issue(j, j % self.n, self.pieces[j])
            self.next += 1
        return i % self.n


def build_program(ntiles, depth=DEPTH, tiles_per_seq=SEQ // T, do_prepass=True):
    ntok = ntiles * T
    nc = bass.Bass("TRN2", target_bir_lowering=False)
    x_d = nc.dram_tensor("x_t", [KC, 128, ntok], F32, kind="ExternalInput").ap()
    p_d = nc.dram_tensor("p_t", [DEPTH, 2, 128, ntok], F32, kind="ExternalInput").ap()
    w_d = nc.dram_tensor("wts", [NW // PP, 128, 2048], F32, kind="ExternalInput").ap()
    prm_d = nc.dram_tensor("prm", [128, NPRM], F32, kind="ExternalInput").ap()
    sgb_d = nc.dram_tensor("sgb", [1, 2 * 8 * 128], F32, kind="ExternalInput").ap()
    cw4_d = nc.dram_tensor("cw4", [128, 2 * EC * 32], F32, kind="ExternalInput").ap()
    out_d = nc.dram_tensor("out_t", [KC, 128, ntok], F32, kind="ExternalOutput").ap()
    wsc = nc.dram_tensor("wsc", [NW], BF16, kind="Internal").ap()
    wsc_pp = wsc.rearrange("(n p f) -> n p f", p=128, f=2048)
    ybd = nc.dram_tensor("ybd", [2, 128, T + HALO + 2], BF16, kind="Internal").ap()

    import contextlib
    es = contextlib.ExitStack()

    def sb(name, shape, dt):
        return es.enter_context(nc.sbuf_tensor(name, shape, dt))

    NWIN = 4
    NF = 12
    xb = sb("xb", [128, KC, T], F32)
    hb = sb("hb", [128, KC, T], BF16)
    cbuf = sb("cbuf", [128, EC, T], F32)
    ub = sb("ub", [128, EC, T], BF16)
    YW = T + HALO + 2
    ybuf = sb("ybuf", [128, 2, YW], BF16)
    halo = sb("halo", [128, 2, EC, HALO], BF16)
    y4 = sb("y4", [128, 2, 4, T + 8], BF16)
    w4 = sb("w4", [128, 2, 32, 32], BF16)
    mask32 = sb("mask32", [128, 32], BF16)
    cw4 = sb("cw4_sb", [128, 2 * EC * 32], F32)
    fs = sb("fs", [128, NF, T], F32)
    bs = sb("bs", [128, 8, T], BF16)
    st = sb("st", [128, 4, T], F32)
    pbuf = sb("pbuf", [128, 2, T], F32)
    pb16 = sb("pb16", [128, 2, T], BF16)
    win = sb("win", [128, NWIN, 2048], BF16)
    wout = sb("wout", [128, 2, EC * 128], BF16)
    wgb = sb("wgb", [128, 2, KC * 128], BF16)
    wpb = sb("wpb", [128, 2, 2 * D], BF16)
    wtb = sb("wtb", [128, 2 * 8 * 128], BF16)
    bbc = sb("bbc", [128, 2 * 8 * 128], F32)
    prm = sb("prm_sb", [128, NPRM], F32)
    ident = sb("ident", [128, 128], BF16)
    onesD = sb("onesD", [128, 128], BF16)
    onesE = sb("onesE", [128, 128], BF16)
    onesf = sb("onesf", [128, 128], F32)
    epsb = sb("epsb", [128, 1], F32)
    ps = [es.enter_context(nc.psum_tensor("ps%d" % i, [128, T], F32)) for i in range(8)]

    sch = Sched()
    add = sch.add

    def prm_col(name, idx):
        o = PRM[name] + idx
        return prm[:, o:o + 1]

    add("sp", lambda e: e.dma_start(out=prm[:], in_=prm_d), writes=["prm"], dma_key="setup")
    add("sp", lambda e: e.dma_start(out=bbc[:], in_=sgb_d.partition_broadcast(128)), writes=["bbc"], dma_key="setup2")
    add("pool", lambda e: e.memset(onesf[:], 1.0), writes=["onesf"])
    add("pool", lambda e: e.memset(onesD[:], 1.0 / D), writes=["onesD"])
    add("pool", lambda e: e.memset(epsb[:], EPS), writes=["epsb"])
    add("pool", lambda e: e.memset(onesE[:], 1.0 / E), writes=["onesE"])
    add("pool", lambda e: e.affine_select(out=ident[:], in_=onesf[:], pattern=[[-1, 128]],
                                          compare_op=ALU.is_equal, fill=0.0, base=0, channel_multiplier=1),
        reads=["onesf"], writes=["ident"])
    add("sp", lambda e: e.dma_start(out=cw4[:], in_=cw4_d), writes=["cw4"], dma_key="setup4")
    add("pool", lambda e: e.tensor_tensor(out=mask32[:], in0=ident[:, 0:32], in1=ident[:, 32:64], op=ALU.add),
        reads=["ident"], writes=["mask32"])
    add("pool", lambda e: e.tensor_tensor(out=mask32[:], in0=mask32[:], in1=ident[:, 64:96], op=ALU.add),
        reads=["ident", "mask32"], writes=["mask32"])
    add("pool", lambda e: e.tensor_tensor(out=mask32[:], in0=mask32[:], in1=ident[:, 96:128], op=ALU.add),
        reads=["ident", "mask32"], writes=["mask32"])
    add("pool", lambda e: e.memset(ybuf[:, :, T + HALO:YW], 0.0), writes=["ypad"])
    sch.freeze("prm", "bbc", "onesD", "onesE", "ident", "onesf", "epsb", "cw4", "mask32", "ypad")

    cst = cbuf[:].rearrange("p (s c) t -> p s (c t)", s=4)
    ust = ub[:].rearrange("p (s c) t -> p s (c t)", s=4)
    npp = NW // PP
    if do_prepass:
        cast_engs = ("act", "dve")
        for i in range(npp):
            s = i % 4
            ckeys = [("cbuf", 4 * s + q) for q in range(4)]
            ukeys = [("ub", 4 * s + q) for q in range(4)]
            add("sp", lambda e, i=i, s=s: e.dma_start(out=cst[:, s, :], in_=w_d[i]),
                writes=ckeys, dma_key=("ppl", s))
            ce = cast_engs[i % 2]
            if ce == "act":
                add("act", lambda e, s=s: e.copy(out=ust[:, s, :], in_=cst[:, s, :]), reads=ckeys, writes=ukeys)
            else:
                add(ce, lambda e, s=s: e.tensor_copy(out=ust[:, s, :], in_=cst[:, s, :]), reads=ckeys, writes=ukeys)
            add("sp", lambda e, i=i, s=s: e.dma_start(out=wsc_pp[i], in_=ust[:, s, :]),
                reads=ukeys, writes=[("wsc", i)], dma_key=("pps", s))
    all_wsc = [("wsc", i) for i in range(npp)]

    def wsc_view(off, n):
        return wsc[off:off + n].rearrange("(p f) -> p f", p=128)

    add("sp", lambda e: e.dma_start(out=wtb[:], in_=wsc_view(WSGU_OFF, 128 * 2048)),
        reads=all_wsc, writes=["wtb"], dma_key="setup3")
    wtb3 = wtb[:].rearrange("p (a t) -> p a t", t=128)
    add("pool", lambda e: e.affine_select(out=wtb3, in_=wtb3, pattern=[[0, 16], [1, 128]],
                                          compare_op=ALU.is_ge, fill=0.0, base=0, channel_multiplier=-1),
        reads=["wtb"], writes=["wtb"])
    sch.freeze("wtb")

    win_pieces, wout_pieces, wg_pieces, wp_pieces, p_pieces = [], [], [], [], []
    for ti in range(ntiles):
        for l in range(depth):
            for ph in (1, 2):
                for c in range(EC):
                    win_pieces.append(WIN_OFF[(l, ph, c)])
            for dc in range(KC):
                wout_pieces.append(WOUT_OFF[(l, dc)])
                wg_pieces.append(WG_OFF[(l, dc)])
            wp_pieces.append(WP_OFF[l])
            p_pieces.append((ti, l))

    def issue_win(j, s, pc):
        off, nb = pc
        add("sp", lambda e: e.dma_start(out=win[:, s, 0:nb * 1024], in_=wsc_view(off, BLK * nb)),
            reads=all_wsc if j < NWIN else (), writes=[("win", s)], dma_key=("win", s))

    def issue_wout(j, s, off):
        add("sp", lambda e: e.dma_start(out=wout[:, s, :], in_=wsc_view(off, 128 * 2048)),
            reads=all_wsc if j < 2 else (), writes=[("wout", s)], dma_key=("wout", s))

    def issue_wg(j, s, off):
        add("sp", lambda e: e.dma_start(out=wgb[:, s, :], in_=wsc_view(off, 128 * 1024)),
            reads=all_wsc if j < 2 else (), writes=[("wg", s)], dma_key=("wg", s))

    def issue_wp(j, s, off):
        add("sp", lambda e: e.dma_start(out=wpb[:, s, :], in_=wsc_view(off, 128 * 2048)),
            reads=all_wsc if j < 2 else (), writes=[("wp", s)], dma_key=("wp", s))

    def issue_p(j, s, pc):
        ti, l = pc
        add("sp", lambda e: e.dma_start(out=pbuf[:],
                                        in_=p_d[l, :, :, ti * T:(ti + 1) * T].rearrange("k p t -> p k t")),
            writes=["pbuf"], dma_key="pld")

    S_win = Stream(sch, "win", NWIN, win_pieces, issue_win)
    S_wout = Stream(sch, "wout", 2, wout_pieces, issue_wout)
    S_wg = Stream(sch, "wg", 2, wg_pieces, issue_wg)
    S_wp = Stream(sch, "wp", 2, wp_pieces, issue_wp)
    S_p = Stream(sch, "p", 1, p_pieces, issue_p)

    rot = {"f": 0, "fl": 0, "b": 0, "y": 0, "dg": 0, "psm": 0, "psc": 0}

    NFL = 6

    def fnew():
        i = NFL + rot["f"] % (NF - NFL)
        rot["f"] += 1
        return i

    def fnew_long():
        i = rot["fl"] % NFL
        rot["fl"] += 1
        return i

    def bnew():
        i = rot["b"] % 8
        rot["b"] += 1
        return i

    def psm_new():
        i = rot["psm"] % 4
        rot["psm"] += 1
        return i

    def psc_new():
        i = 4 + rot["psc"] % 2
        rot["psc"] += 1
        return i

    PS_SUM, PS_SQ = 6, 7

    def mm(out, lhsT, rhs, start, stop, reads, bank, tp=None):
        if tp is None:
            add("pe", lambda e: e.matmul(out, lhsT, rhs, start=start, stop=stop), reads=reads, writes=[("ps", bank)])
        else:
            add("pe", lambda e: e.matmul(out, lhsT, rhs, start=start, stop=stop, tile_position=tp),
                reads=reads, writes=[("ps", bank)])

    rms_state = {"n": 0, "pend": None}

    def rms_flush():
        pd = rms_state["pend"]
        if pd is not None:
            b, first, last = pd
            mm(ps[PS_SUM][:], onesD[:], bs[:, b, :], first, last, [("bs", b), "onesD"], PS_SUM)
            rms_state["pend"] = None

    def rms_partial(kc):
        rms_flush()
        b = bnew()
        add("act", lambda e: e.activation(out=bs[:, b, :], in_=xb[:, kc, :], func=AF.Square),
            reads=[("x", kc)], writes=[("bs", b)])
        n = rms_state["n"]
        rms_state["pend"] = (b, n == 0, n == KC - 1)
        rms_state["n"] = (n + 1) % KC

    def rms_finalize():
        rms_flush()
        assert rms_state["n"] == 0
        add("act", lambda e: e.activation(out=st[:, 2, :], in_=ps[PS_SUM][:], func=AF.Ln, bias=epsb[:], scale=1.0),
            reads=[("ps", PS_SUM), "epsb"], writes=[("st", 2)])
        add("act", lambda e: e.activation(out=st[:, 2, :], in_=st[:, 2, :], func=AF.Exp, scale=-0.5),
            reads=[("st", 2)], writes=[("st", 2)])

    def h_from_x(gname, l):
        rms_finalize()
        for kc in range(KC):
            eng = "dve"
            add(eng, lambda e, kc=kc: e.scalar_tensor_tensor(
                out=hb[:, kc, :], in0=xb[:, kc, :], scalar=prm_col(gname, l * KC + kc), in1=st[:, 2, :],
                op0=ALU.mult, op1=ALU.mult),
                reads=[("x", kc), ("st", 2), "prm"], writes=[("h", kc)])

    def proj_block(bank, slot, blk):
        for kc in range(KC):
            o = blk * 1024 + kc * 128
            mm(ps[bank][:], win[:, slot, o:o + 128], hb[:, kc, :], kc == 0, kc == KC - 1,
               [("h", kc), ("win", slot)], bank)

    def pipeline(n, stages):
        ns = len(stages)
        for i in range(n + ns - 1):
            for k, stg in enumerate(stages):
                c = i - k
                if 0 <= c < n:
                    stg(c)

    def layer(ti, l, widx, last_layer):
        j = l // 2
        conv = (l % 2 == 0)
        seq_start = (ti % tiles_per_seq == 0)
        h_from_x("ng", l)
        cst_ = [dict() for _ in range(EC)]

        def stats_prep(c):
            b1, b2 = bnew(), bnew()
            add("act", lambda e: e.activation(out=bs[:, b1, :], in_=cbuf[:, c, :], func=AF.Square),
                reads=[("cbuf", c)], writes=[("bs", b1)])
            add("dve", lambda e: e.tensor_copy(out=bs[:, b2, :], in_=cbuf[:, c, :]),
                reads=[("cbuf", c)], writes=[("bs", b2)])
            cst_[c]["b1"], cst_[c]["b2"] = b1, b2

        def stats_mm(c):
            b1, b2 = cst_[c]["b1"], cst_[c]["b2"]
            mm(ps[PS_SUM][:], onesE[:], bs[:, b2, :], c == 0, c == EC - 1, [("bs", b2), "onesE"], PS_SUM)
            mm(ps[PS_SQ][:], onesE[:], bs[:, b1, :], c == 0, c == EC - 1, [("bs", b1), "onesE"], PS_SQ)

        if conv:
            def sA(c):
                slot = S_win.get(widx[0])
                widx[0] += 1
                ba, bb_ = psm_new(), psm_new()
                proj_block(ba, slot, 0)
                proj_block(bb_, slot, 1)
                f = fnew()
                add("act", lambda e: e.activation(out=fs[:, f, :], in_=ps[bb_][:], func=AF.Sigmoid),
                    reads=[("ps", bb_)], writes=[("fs", f)])
                y = rot["y"] % 2
                rot["y"] += 1
                if seq_start:
                    add("pool", lambda e: e.memset(ybuf[:, y, 0:HALO], 0.0), writes=[("yh", y)])
                else:
                    add("pool", lambda e: e.tensor_copy(out=ybuf[:, y, 0:HALO], in_=halo[:, j, c, :]),
                        reads=[("halo", j, c)], writes=[("yh", y)])
                add("dve", lambda e: e.tensor_tensor(out=ybuf[:, y, HALO:HALO + T], in0=ps[ba][:],
                                                     in1=fs[:, f, :], op=ALU.mult),
                    reads=[("ps", ba), ("fs", f)], writes=[("y", y)])
                add("pool", lambda e: e.tensor_copy(out=halo[:, j, c, :], in_=ybuf[:, y, T:T + HALO]),
                    reads=[("y", y)], writes=[("halo", j, c)])
                dg = rot["dg"] % 2
                rot["dg"] += 1
                co = (j * EC + c) * 32
                add("dve", lambda e: e.tensor_tensor(
                    out=w4[:, dg, :, :], in0=cw4[:, co:co + 32].unsqueeze(2).to_broadcast([128, 32, 32]),
                    in1=mask32[:].unsqueeze(1).to_broadcast([128, 32, 32]), op=ALU.mult),
                    reads=["cw4", "mask32"], writes=[("w4", dg)])
                par = (c // 2) % 2
                add("sp", lambda e: e.dma_start(out=ybd[dg], in_=ybuf[:, y, :]),
                    reads=[("y", y), ("yh", y), "ypad"], writes=[("ybd", dg)], dma_key=("ybd", dg))
                for q in range(4):
                    add("sp", lambda e, q=q: e.dma_start(
                        out=y4[32 * q:32 * q + 32, dg, :, :],
                        in_=ybd[dg, :, 8 * q:8 * q + T + 8].rearrange("(g c) u -> c g u", g=4)),
                        reads=[("ybd", dg)], writes=[("y4", dg, q)], dma_key=("y4", dg, par))
                cst_[c]["y"], cst_[c]["dg"] = y, dg

            def sB(c):
                y, dg = cst_[c]["y"], cst_[c]["dg"]
                bc = psc_new()
                y4keys = [("y4", dg, q) for q in range(4)]
                for r in range(8):
                    for g in range(4):
                        mm(ps[bc][32 * g:32 * g + 32, :], w4[:, dg, g * 8 + r, :], y4[:, dg, g, r:r + T], r == 0, r == 7,
                           [("w4", dg)] + y4keys, bc, tp=(0, 32 * g))
                cwo = PRM["cw"] + (j * EC + c) * KW
                if KD > 0:
                    fa = fnew()
                    k0 = DTAPS[0]
                    add("dve", lambda e: e.tensor_scalar(out=fs[:, fa, :], in0=ybuf[:, y, k0:k0 + T], scalar1=prm[:, cwo + k0:cwo + k0 + 1],
                                                         scalar2=None, op0=ALU.mult),
                        reads=[("y", y), ("yh", y), "prm"], writes=[("fs", fa)])
                    for k in DTAPS[1:]:
                        add("dve", lambda e, k=k: e.scalar_tensor_tensor(
                            out=fs[:, fa, :], in0=ybuf[:, y, k:k + T], scalar=prm[:, cwo + k:cwo + k + 1], in1=fs[:, fa, :],
                            op0=ALU.mult, op1=ALU.add),
                            reads=[("y", y), ("yh", y), "prm", ("fs", fa)], writes=[("fs", fa)])
                    add("dve", lambda e: e.scalar_tensor_tensor(
                        out=cbuf[:, c, :], in0=ps[bc][:], scalar=prm_col("cb", j * EC + c), in1=fs[:, fa, :],
                        op0=ALU.add, op1=ALU.add),
                        reads=[("ps", bc), "prm", ("fs", fa)], writes=[("cbuf", c)])
                else:
                    add("act", lambda e: e.activation(out=cbuf[:, c, :], in_=ps[bc][:], func=AF.Identity,
                                                      bias=prm_col("cb", j * EC + c), scale=1.0),
                        reads=[("ps", bc), "prm"], writes=[("cbuf", c)])
                stats_prep(c)

            pipeline(EC, [sA, sB, stats_mm])
        else:
            def sA(c):
                slot = S_win.get(widx[0])
                widx[0] += 1
                bb_ = psm_new()
                proj_block(bb_, slot, 0)
                add("act", lambda e: e.activation(out=cbuf[:, c, :], in_=ps[bb_][:], func=AF.Gelu),
                    reads=[("ps", bb_)], writes=[("cbuf", c)])
                stats_prep(c)

            pipeline(EC, [sA, stats_mm])
        add("act", lambda e: e.activation(out=st[:, 3, :], in_=ps[PS_SUM][:], func=AF.Square),
            reads=[("ps", PS_SUM)], writes=[("st", 3)])
        add("dve", lambda e: e.tensor_tensor(out=st[:, 1, :], in0=ps[PS_SQ][:], in1=st[:, 3, :], op=ALU.subtract),
            reads=[("ps", PS_SQ), ("st", 3)], writes=[("st", 1)])
        add("act", lambda e: e.activation(out=st[:, 1, :], in_=st[:, 1, :], func=AF.Ln, bias=epsb[:], scale=1.0),
            reads=[("st", 1), "epsb"], writes=[("st", 1)])
        add("act", lambda e: e.activation(out=st[:, 1, :], in_=st[:, 1, :], func=AF.Exp, scale=-0.5),
            reads=[("st", 1)], writes=[("st", 1)])
        add("dve", lambda e: e.scalar_tensor_tensor(out=st[:, 0, :], in0=ps[PS_SUM][:], scalar=-1.0, in1=st[:, 1, :],
                                                    op0=ALU.mult, op1=ALU.mult),
            reads=[("ps", PS_SUM), ("st", 1)], writes=[("st", 0)])
        gname, bname = ("clg", "clb") if conv else ("slg", "slb")

        def ln_apply(c):
            add("pool", lambda e: e.tensor_tensor(out=cbuf[:, c, :], in0=cbuf[:, c, :], in1=st[:, 1, :], op=ALU.mult),
                reads=[("cbuf", c), ("st", 1)], writes=[("cbuf", c)])
            add("dve", lambda e: e.tensor_tensor(out=cbuf[:, c, :], in0=cbuf[:, c, :], in1=st[:, 0, :], op=ALU.add),
                reads=[("cbuf", c), ("st", 0)], writes=[("cbuf", c)])

        if conv:
            def s2a(c):
                slot = S_win.get(widx[0])
                widx[0] += 1
                bz = psm_new()
                proj_block(bz, slot, 0)
                f3 = fnew_long()
                add("act", lambda e: e.activation(out=fs[:, f3, :], in_=ps[bz][:], func=AF.Silu),
                    reads=[("ps", bz)], writes=[("fs", f3)])
                cst_[c]["f3"] = f3

            def s2b(c):
                ln_apply(c)

            def s2c(c):
                f3 = cst_[c]["f3"]
                gcol, bcol = prm_col(gname, j * EC + c), prm_col(bname, j * EC + c)
                f2 = fnew()
                add("act", lambda e: e.activation(out=fs[:, f2, :], in_=cbuf[:, c, :], func=AF.Silu, bias=bcol, scale=gcol),
                    reads=[("cbuf", c), "prm"], writes=[("fs", f2)])
                add("dve", lambda e: e.tensor_tensor(out=ub[:, c, :], in0=fs[:, f2, :], in1=fs[:, f3, :], op=ALU.mult),
                    reads=[("fs", f2), ("fs", f3)], writes=[("ub", c)])

            pipeline(EC, [s2a, s2b, s2c])
        else:
            def s2a(c):
                slot = S_win.get(widx[0])
                widx[0] += 1
                ba, bz = psm_new(), psm_new()
                proj_block(ba, slot, 0)
                proj_block(bz, slot, 1)
                f2, f3 = fnew_long(), fnew()
                add("act", lambda e: e.activation(out=fs[:, f2, :], in_=ps[ba][:], func=AF.Gelu),
                    reads=[("ps", ba)], writes=[("fs", f2)])
                add("act", lambda e: e.activation(out=fs[:, f3, :], in_=ps[bz][:], func=AF.Silu),
                    reads=[("ps", bz)], writes=[("fs", f3)])
                add("pool", lambda e: e.tensor_tensor(out=fs[:, f2, :], in0=fs[:, f2, :], in1=fs[:, f3, :], op=ALU.mult),
                    reads=[("fs", f2), ("fs", f3)], writes=[("fs", f2)])
                cst_[c]["f2"] = f2

            def sL(c):
                ln_apply(c)

            def sV(c):
                gcol, bcol = prm_col(gname, j * EC + c), prm_col(bname, j * EC + c)
                bv = bnew()
                add("act", lambda e: e.activation(out=bs[:, bv, :], in_=cbuf[:, c, :], func=AF.Identity, bias=bcol, scale=gcol),
                    reads=[("cbuf", c), "prm"], writes=[("bs", bv)])
                cst_[c]["bv"] = bv

            def s2b2(c):
                bv = cst_[c]["bv"]
                bt = psc_new()
                for n in range(4):
                    mm(ps[bt][:, n * 128:(n + 1) * 128], bs[:, bv, n * 128:(n + 1) * 128], ident[:], True, True,
                       [("bs", bv), "ident"], bt)
                bT = bnew()
                add("dve", lambda e: e.tensor_copy(out=bs[:, bT, :], in_=ps[bt][:]),
                    reads=[("ps", bt)], writes=[("bs", bT)])
                cst_[c]["bT"] = bT

            def s2c(c):
                bT, f2 = cst_[c]["bT"], cst_[c]["f2"]
                g = c // 2
                bm = PS_SUM + (c % 2)
                wo = (j * 8 + g) * 128
                for n in range(4):
                    mm(ps[bm][:, n * 128:(n + 1) * 128], bs[:, bT, n * 128:(n + 1) * 128], wtb[:, wo:wo + 128],
                       True, True, [("bs", bT), "wtb"], bm)
                f4 = fnew()
                add("dve", lambda e: e.tensor_tensor(
                    out=fs[:, f4, :].rearrange("p (n t) -> p n t", n=4),
                    in0=ps[bm][:].rearrange("p (n t) -> p n t", n=4),
                    in1=bbc[:, wo:wo + 128].unsqueeze(1).to_broadcast([128, 4, 128]), op=ALU.add),
                    reads=[("ps", bm), "bbc"], writes=[("fs", f4)])
                add("dve", lambda e: e.tensor_tensor(out=ub[:, c, :], in0=fs[:, f4, :], in1=fs[:, f2, :], op=ALU.mult),
                    reads=[("fs", f4), ("fs", f2)], writes=[("ub", c)])

            nop = lambda c: None
            pipeline(EC, [sL, s2a, sV, s2b2, s2c])
        for dc in range(KC):
            s = S_wout.get(widx[1])
            widx[1] += 1
            bo = psm_new()
            for ec in range(EC):
                mm(ps[bo][:], wout[:, s, ec * 128:(ec + 1) * 128], ub[:, ec, :], ec == 0, ec == EC - 1,
                   [("ub", ec), ("wout", s)], bo)
            add("dve", lambda e, dc=dc, bo=bo: e.tensor_tensor(out=xb[:, dc, :], in0=xb[:, dc, :], in1=ps[bo][:], op=ALU.add),
                reads=[("x", dc), ("ps", bo)], writes=[("x", dc)])
            rms_partial(dc)
        S_p.get(widx[3])
        sp_ = S_wp.get(widx[3])
        widx[3] += 1
        add("pool", lambda e: e.tensor_copy(out=pb16[:], in_=pbuf[:]), reads=["pbuf"], writes=["pb16"])
        h_from_x("png", l)
        for dc in range(KC):
            s = S_wg.get(widx[2])
            widx[2] += 1
            bp = psc_new()
            for kc in range(2):
                o = kc * D + dc * 128
                mm(ps[bp][:], wpb[:, sp_, o:o + 128], pb16[:, kc, :], kc == 0, kc == 1, ["pb16", ("wp", sp_)], bp)
            bg = psm_new()
            for kc in range(KC):
                mm(ps[bg][:], wgb[:, s, kc * 128:(kc + 1) * 128], hb[:, kc, :], kc == 0, kc == KC - 1,
                   [("h", kc), ("wg", s)], bg)
            f = fnew()
            add("act", lambda e, f=f, bg=bg: e.activation(out=fs[:, f, :], in_=ps[bg][:], func=AF.Sigmoid),
                reads=[("ps", bg)], writes=[("fs", f)])
            add("dve", lambda e, f=f, bp=bp: e.tensor_tensor(out=fs[:, f, :], in0=fs[:, f, :], in1=ps[bp][:], op=ALU.mult),
                reads=[("fs", f), ("ps", bp)], writes=[("fs", f)])
            add("dve", lambda e, f=f, dc=dc: e.tensor_tensor(out=xb[:, dc, :], in0=xb[:, dc, :], in1=fs[:, f, :], op=ALU.add),
                reads=[("fs", f), ("x", dc)], writes=[("x", dc)])
            rms_partial(dc)

    widx = [0, 0, 0, 0]
    out_keys = []
    for kc in range(KC):
        add("sp", lambda e, kc=kc: e.dma_start(out=xb[:, kc, :], in_=x_d[kc, :, 0:T]),
            writes=[("x", kc)], dma_key=("xl", kc))
        rms_partial(kc)
    for ti in range(ntiles):
        for l in range(depth):
            layer(ti, l, widx, l == depth - 1)
        rms_finalize()
        for kc in range(KC):
            def fin(kc=kc, ti=ti):
                add("dve", lambda e: e.scalar_tensor_tensor(
                    out=cbuf[:, kc, :], in0=xb[:, kc, :], scalar=prm_col("fg", kc), in1=st[:, 2, :],
                    op0=ALU.mult, op1=ALU.mult),
                    reads=[("x", kc), ("st", 2), "prm"], writes=[("cbuf", kc)])
                add("act", lambda e: e.dma_start(out=out_d[kc, :, ti * T:(ti + 1) * T], in_=cbuf[:, kc, :]),
                    reads=[("cbuf", kc)], writes=[("out", ti, kc)], dma_key=("ost", kc))
                if ("ost", kc) not in out_keys:
                    out_keys.append(("ost", kc))
                if ti + 1 < ntiles:
                    add("sp", lambda e: e.dma_start(out=xb[:, kc, :], in_=x_d[kc, :, (ti + 1) * T:(ti + 2) * T]),
                        writes=[("x", kc)], dma_key=("xl", kc))
                    rms_partial(kc)
            fin()

    sch.emit(nc, out_keys)
    es.close()
    return nc


_NC_CACHE = {}


def kernel(x, p, norm_g, w_in, w_out, conv_w, conv_b, conv_ln_g, conv_ln_b,
           sgu_ln_g, sgu_ln_b, sgu_w, sgu_b, pl_norm_g, pl_gate_w, pl_proj_w, final_g):
    x = np.asarray(x, np.float32)
    p = np.asarray(p, np.float32)
    B, S, _ = x.shape
    bpc = B // NCORES
    ntok = bpc * S
    ntiles = ntok // T
    wts = _host_weights(np.asarray(w_in, np.float32), np.asarray(w_out, np.float32),
                        np.asarray(pl_gate_w, np.float32), np.asarray(pl_proj_w, np.float32),
                        np.asarray(sgu_w, np.float32)).reshape(NW // PP, 128, 2048)
    prm = _host_params(*[np.asarray(a, np.float32) for a in
                         (norm_g, pl_norm_g, final_g, conv_w, conv_b, conv_ln_g, conv_ln_b, sgu_ln_g, sgu_ln_b)])
    sgb = np.ascontiguousarray(np.asarray(sgu_b, np.float32).reshape(1, -1))
    cw4 = _host_cw4(np.asarray(conv_w, np.float32))
    in_maps = []
    for c in range(NCORES):
        xc = x[c * bpc:(c + 1) * bpc].reshape(ntok, D)
        x_t = np.ascontiguousarray(xc.T).reshape(KC, 128, ntok)
        pc = p[:, c * bpc:(c + 1) * bpc].reshape(DEPTH, ntok, PLE)
        p_t = np.ascontiguousarray(pc.transpose(0, 2, 1)).reshape(DEPTH, 2, 128, ntok)
        in_maps.append({"x_t": x_t, "p_t": p_t, "wts": wts, "prm": prm, "sgb": sgb, "cw4": cw4})
    nc = build_program(ntiles, DEPTH, S // T)
    res = run_bass_kernel_spmd(nc, in_maps, core_ids=list(range(NCORES)))
    out = np.empty((B, S, D), np.float32)
    for c in range(NCORES):
        o = np.asarray(res.results[c]["out_t"]).reshape(D, ntok)
        out[c * bpc:(c + 1) * bpc] = o.T.reshape(bpc, S, D)
    return out
```

```python
import numpy as np
import concourse.bass as bass
import concourse.mybir as mybir
from concourse.bass_utils import run_bass_kernel_spmd

F32 = mybir.dt.float32
BF16 = mybir.dt.bfloat16
AF = mybir.ActivationFunctionType
ALU = mybir.AluOpType

D = 1024
E = 2048
DEPTH = 4
KW = 31
PLE = 256
NCORES = 8
SEQ = 4096
T = 512
EPS = 1e-6
KC = D // 128
EC = E // 128
HALO = KW - 1
KD = 0
DTAPS = [2 * i for i in range(KD)]
PTAPS = [k for k in range(KW) if k not in DTAPS]

BLK = 128 * 1024
PP = 128 * 2048


def _win_pieces(l):
    if l % 2 == 0:
        p1 = [[(0, c), (1, c)] for c in range(EC)]
        p2 = [[(2, c)] for c in range(EC)]
    else:
        p1 = [[(1, c)] for c in range(EC)]
        p2 = [[(0, c), (2, c)] for c in range(EC)]
    return p1, p2


def _weight_offsets():
    off = 0
    win = {}
    for l in range(DEPTH):
        p1, p2 = _win_pieces(l)
        for ph, pcs in ((1, p1), (2, p2)):
            for c, blks in enumerate(pcs):
                win[(l, ph, c)] = (off, len(blks))
                off += BLK * len(blks)
    wout = {}
    for l in range(DEPTH):
        for dc in range(KC):
            wout[(l, dc)] = off
            off += 128 * EC * 128
    wg = {}
    for l in range(DEPTH):
        for dc in range(KC):
            wg[(l, dc)] = off
            off += 128 * KC * 128
    wp = {}
    for l in range(DEPTH):
        wp[l] = off
        off += 128 * 2 * D
    wsgu = off
    off += 128 * 2 * 8 * 128
    assert off % PP == 0
    return win, wout, wg, wp, wsgu, off


WIN_OFF, WOUT_OFF, WG_OFF, WP_OFF, WSGU_OFF, NW = _weight_offsets()

PRM = {}
_o = 0
for _n, _sz in (("ng", DEPTH * KC), ("png", DEPTH * KC), ("fg", KC), ("cw", 2 * EC * KW),
                ("cb", 2 * EC), ("clg", 2 * EC), ("clb", 2 * EC), ("slg", 2 * EC), ("slb", 2 * EC)):
    PRM[_n] = _o
    _o += _sz
NPRM = _o


def _host_weights(w_in, w_out, pl_gate_w, pl_proj_w, sgu_w):
    parts = []
    for l in range(DEPTH):
        wl = w_in[l].reshape(KC, 128, 3, EC, 128)
        p1, p2 = _win_pieces(l)
        for pcs in (p1, p2):
            for blks in pcs:
                bl = [wl[:, :, j, c, :].transpose(1, 0, 2).reshape(128, 1024) for (j, c) in blks]
                parts.append(np.ascontiguousarray(np.stack(bl, axis=1)).reshape(-1))
    for l in range(DEPTH):
        wl = w_out[l].reshape(EC, 128, KC, 128)
        parts.append(np.ascontiguousarray(wl.transpose(2, 1, 0, 3)).reshape(-1))
    for l in range(DEPTH):
        wl = pl_gate_w[l].reshape(KC, 128, KC, 128)
        parts.append(np.ascontiguousarray(wl.transpose(2, 1, 0, 3)).reshape(-1))
    for l in range(DEPTH):
        wl = pl_proj_w[l].reshape(2, 128, D)
        parts.append(np.ascontiguousarray(wl.transpose(1, 0, 2)).reshape(-1))
    parts.append(np.ascontiguousarray(sgu_w.transpose(3, 0, 1, 2)).reshape(-1))
    flat = np.concatenate(parts).astype(np.float32, copy=False)
    assert flat.size == NW
    return flat


def _host_cw4(conv_w):
    wpad = np.zeros((2, 32, E), np.float32)
    wpad[:, :KW] = conv_w
    a = wpad.reshape(2, 4, 8, EC, 4, 32)
    return np.ascontiguousarray(a.transpose(1, 5, 0, 3, 4, 2)).reshape(128, 2 * EC * 32)


def _host_params(norm_g, pl_norm_g, final_g, conv_w, conv_b, conv_ln_g, conv_ln_b, sgu_ln_g, sgu_ln_b):
    prm = np.zeros((128, NPRM), np.float32)

    def put(name, arr):
        prm[:, PRM[name]:PRM[name] + arr.shape[1]] = arr

    put("ng", norm_g.reshape(DEPTH, KC, 128).transpose(2, 0, 1).reshape(128, -1))
    put("png", pl_norm_g.reshape(DEPTH, KC, 128).transpose(2, 0, 1).reshape(128, -1))
    put("fg", final_g.reshape(KC, 128).T)
    put("cw", conv_w.reshape(2, KW, EC, 128).transpose(3, 0, 2, 1).reshape(128, -1))
    for nm, a in (("cb", conv_b), ("clg", conv_ln_g), ("clb", conv_ln_b), ("slg", sgu_ln_g), ("slb", sgu_ln_b)):
        put(nm, a.reshape(2, EC, 128).transpose(2, 0, 1).reshape(128, -1))
    return prm


class Op:
    __slots__ = ("eng", "fn", "deps", "signal", "is_dma", "dma_key", "sig")

    def __init__(self, eng, fn, is_dma, dma_key):
        self.eng = eng
        self.fn = fn
        self.deps = []
        self.signal = is_dma
        self.is_dma = is_dma
        self.dma_key = dma_key
        self.sig = None


class Sched:
    ENGS = ("pe", "act", "dve", "pool", "sp")

    def __init__(self):
        self.ops = {e: [] for e in self.ENGS}
        self.last_w = {}
        self.readers = {}
        self.frozen = set()
        self.dma_keys = []

    def add(self, eng, fn, reads=(), writes=(), dma_key=None):
        is_dma = dma_key is not None
        op = Op(eng, fn, is_dma, dma_key)
        if is_dma and dma_key not in self.dma_keys:
            self.dma_keys.append(dma_key)
        deps = {}
        for k in reads:
            w = self.last_w.get(k)
            if w is not None:
                deps[id(w)] = w
        for k in writes:
            assert k not in self.frozen, k
            w = self.last_w.get(k)
            if w is not None:
                deps[id(w)] = w
            for r in self.readers.get(k, ()):
                deps[id(r)] = r
        for d in deps.values():
            if d.eng == "pe" and eng == "pe" and not d.is_dma and not is_dma:
                continue
            op.deps.append(d)
            d.signal = True
        for k in reads:
            if k not in self.frozen:
                self.readers.setdefault(k, []).append(op)
        for k in writes:
            self.last_w[k] = op
            self.readers[k] = []
        self.ops[eng].append(op)
        return op

    def freeze(self, *keys):
        self.frozen.update(keys)

    def emit(self, nc, final_wait_keys):
        import contextlib
        with contextlib.ExitStack() as es:
            esem = {e: es.enter_context(nc.semaphore("s_" + e)) for e in self.ENGS}
            dsem = {k: es.enter_context(nc.semaphore("d_" + "".join(ch for ch in str(k) if ch.isalnum()))) for k in self.dma_keys}
            cnt = {e: 0 for e in self.ENGS}
            dcnt = {k: 0 for k in self.dma_keys}
            for e in self.ENGS:
                for op in self.ops[e]:
                    if op.is_dma:
                        dcnt[op.dma_key] += 16
                        op.sig = (dsem[op.dma_key], dcnt[op.dma_key])
                    elif op.signal:
                        cnt[e] += 1
                        op.sig = (esem[e], cnt[e])
            block = es.enter_context(nc.Block())

            def run(engname, engobj, final=False):
                waited = {}
                for op in self.ops[engname]:
                    need = {}
                    for d in op.deps:
                        s, v = d.sig
                        if waited.get(id(s), 0) >= v:
                            continue
                        if need.get(id(s), (None, 0))[1] < v:
                            need[id(s)] = (s, v)
                    for s, v in need.values():
                        engobj.wait_ge(s, v)
                        waited[id(s)] = v
                    ins = op.fn(engobj)
                    if op.sig is not None:
                        ins.then_inc(op.sig[0], 16 if op.is_dma else 1)
                if final:
                    for k in final_wait_keys:
                        engobj.wait_ge(dsem[k], dcnt[k])

            @block.tensor
            def _(e):
                run("pe", e)

            @block.scalar
            def _(e):
                run("act", e)

            @block.vector
            def _(e):
                run("dve", e)

            @block.gpsimd
            def _(e):
                run("pool", e)

            @block.sync
            def _(e):
                run("sp", e, final=True)


class Stream:
    def __init__(self, sch, name, nslots, pieces, issue):
        self.sch, self.name, self.n, self.pieces, self.issue = sch, name, nslots, pieces, issue
        self.next = 0

    def get(self, i):
        hi = min(len(self.pieces), i + self.n)
        while self.next < hi:
            j = self.next
            self.issue(j, j % self.n, self.pieces[j])
            self.next += 1
        return i % self.n


def build_program(ntiles, depth=DEPTH, tiles_per_seq=SEQ // T, do_prepass=True):
    ntok = ntiles * T
    nc = bass.Bass("TRN2", target_bir_lowering=False)
    x_d = nc.dram_tensor("x_t", [KC, 128, ntok], F32, kind="ExternalInput").ap()
    p_d = nc.dram_tensor("p_t", [DEPTH, 2, 128, ntok], F32, kind="ExternalInput").ap()
    w_d = nc.dram_tensor("wts", [NW // PP, 128, 2048], F32, kind="ExternalInput").ap()
    prm_d = nc.dram_tensor("prm", [128, NPRM], F32, kind="ExternalInput").ap()
    sgb_d = nc.dram_tensor("sgb", [1, 2 * 8 * 128], F32, kind="ExternalInput").ap()
    cw4_d = nc.dram_tensor("cw4", [128, 2 * EC * 32], F32, kind="ExternalInput").ap()
    out_d = nc.dram_tensor("out_t", [KC, 128, ntok], F32, kind="ExternalOutput").ap()
    wsc = nc.dram_tensor("wsc", [NW], BF16, kind="Internal").ap()
    wsc_pp = wsc.rearrange("(n p f) -> n p f", p=128, f=2048)
    ybd = nc.dram_tensor("ybd", [3, 128, T + HALO + 2], BF16, kind="Internal").ap()

    import contextlib
    es = contextlib.ExitStack()

    def sb(name, shape, dt):
        return es.enter_context(nc.sbuf_tensor(name, shape, dt))

    NWIN = 4
    NF = 12
    xb = sb("xb", [128, KC, T], F32)
    hb = sb("hb", [128, KC, T], BF16)
    cbuf = sb("cbuf", [128, EC, T], F32)
    ub = sb("ub", [128, EC, T], BF16)
    YW = T + HALO + 2
    ybuf = sb("ybuf", [128, 2, YW], BF16)
    halo = sb("halo", [128, 2, EC, HALO], BF16)
    y4 = sb("y4", [128, 3, 4, T + 8], BF16)
    w4 = sb("w4", [128, 3, 32, 32], BF16)
    mask32 = sb("mask32", [128, 32], BF16)
    cw4 = sb("cw4_sb", [128, 2 * EC * 32], F32)
    fs = sb("fs", [128, NF, T], F32)
    bs = sb("bs", [128, 8, T], BF16)
    st = sb("st", [128, 4, T], F32)
    pbuf = sb("pbuf", [128, 2, T], F32)
    pb16 = sb("pb16", [128, 2, T], BF16)
    win = sb("win", [128, NWIN, 2048], BF16)
    wout = sb("wout", [128, 2, EC * 128], BF16)
    wgb = sb("wgb", [128, 2, KC * 128], BF16)
    wpb = sb("wpb", [128, 2, 2 * D], BF16)
    wtb = sb("wtb", [128, 2 * 8 * 128], BF16)
    bbc = sb("bbc", [128, 2 * 8 * 128], F32)
    prm = sb("prm_sb", [128, NPRM], F32)
    ident = sb("ident", [128, 128], BF16)
    onesD = sb("onesD", [128, 128], BF16)
    onesE = sb("onesE", [128, 128], BF16)
    onesf = sb("onesf", [128, 128], F32)
    epsb = sb("epsb", [128, 1], F32)
    ps = [es.enter_context(nc.psum_tensor("ps%d" % i, [128, T], F32)) for i in range(8)]

    sch = Sched()
    add = sch.add

    def prm_col(name, idx):
        o = PRM[name] + idx
        return prm[:, o:o + 1]

    add("sp", lambda e: e.dma_start(out=prm[:], in_=prm_d), writes=["prm"], dma_key="setup")
    add("sp", lambda e: e.dma_start(out=bbc[:], in_=sgb_d.partition_broadcast(128)), writes=["bbc"], dma_key="setup2")
    add("pool", lambda e: e.memset(onesf[:], 1.0), writes=["onesf"])
    add("pool", lambda e: e.memset(onesD[:], 1.0 / D), writes=["onesD"])
    add("pool", lambda e: e.memset(epsb[:], EPS), writes=["epsb"])
    add("pool", lambda e: e.memset(onesE[:], 1.0 / E), writes=["onesE"])
    add("pool", lambda e: e.affine_select(out=ident[:], in_=onesf[:], pattern=[[-1, 128]],
                                          compare_op=ALU.is_equal, fill=0.0, base=0, channel_multiplier=1),
        reads=["onesf"], writes=["ident"])
    add("sp", lambda e: e.dma_start(out=cw4[:], in_=cw4_d), writes=["cw4"], dma_key="setup4")
    add("pool", lambda e: e.tensor_tensor(out=mask32[:], in0=ident[:, 0:32], in1=ident[:, 32:64], op=ALU.add),
        reads=["ident"], writes=["mask32"])
    add("pool", lambda e: e.tensor_tensor(out=mask32[:], in0=mask32[:], in1=ident[:, 64:96], op=ALU.add),
        reads=["ident", "mask32"], writes=["mask32"])
    add("pool", lambda e: e.tensor_tensor(out=mask32[:], in0=mask32[:], in1=ident[:, 96:128], op=ALU.add),
        reads=["ident", "mask32"], writes=["mask32"])
    add("pool", lambda e: e.memset(ybuf[:, :, T + HALO:YW], 0.0), writes=["ypad"])
    sch.freeze("prm", "bbc", "onesD", "onesE", "ident", "onesf", "epsb", "cw4", "mask32", "ypad")

    cst = cbuf[:].rearrange("p (s c) t -> p s (c t)", s=4)
    ust = ub[:].rearrange("p (s c) t -> p s (c t)", s=4)
    npp = NW // PP
    if do_prepass:
        cast_engs = ("act", "dve")
        for i in range(npp):
            s = i % 4
            ckeys = [("cbuf", 4 * s + q) for q in range(4)]
            ukeys = [("ub", 4 * s + q) for q in range(4)]
            add("sp", lambda e, i=i, s=s: e.dma_start(out=cst[:, s, :], in_=w_d[i]),
                writes=ckeys, dma_key=("ppl", s))
            ce = cast_engs[i % 2]
            if ce == "act":
                add("act", lambda e, s=s: e.copy(out=ust[:, s, :], in_=cst[:, s, :]), reads=ckeys, writes=ukeys)
            else:
                add(ce, lambda e, s=s: e.tensor_copy(out=ust[:, s, :], in_=cst[:, s, :]), reads=ckeys, writes=ukeys)
            add("sp", lambda e, i=i, s=s: e.dma_start(out=wsc_pp[i], in_=ust[:, s, :]),
                reads=ukeys, writes=[("wsc", i)], dma_key=("pps", s))
    all_wsc = [("wsc", i) for i in range(npp)]

    def wsc_view(off, n):
        return wsc[off:off + n].rearrange("(p f) -> p f", p=128)

    add("sp", lambda e: e.dma_start(out=wtb[:], in_=wsc_view(WSGU_OFF, 128 * 2048)),
        reads=all_wsc, writes=["wtb"], dma_key="setup3")
    wtb3 = wtb[:].rearrange("p (a t) -> p a t", t=128)
    add("pool", lambda e: e.affine_select(out=wtb3, in_=wtb3, pattern=[[0, 16], [1, 128]],
                                          compare_op=ALU.is_ge, fill=0.0, base=0, channel_multiplier=-1),
        reads=["wtb"], writes=["wtb"])
    sch.freeze("wtb")

    win_pieces, wout_pieces, wg_pieces, wp_pieces, p_pieces = [], [], [], [], []
    for ti in range(ntiles):
        for l in range(depth):
            for ph in (1, 2):
                for c in range(EC):
                    win_pieces.append(WIN_OFF[(l, ph, c)])
            for dc in range(KC):
                wout_pieces.append(WOUT_OFF[(l, dc)])
                wg_pieces.append(WG_OFF[(l, dc)])
            wp_pieces.append(WP_OFF[l])
            p_pieces.append((ti, l))

    def issue_win(j, s, pc):
        off, nb = pc
        add("sp", lambda e: e.dma_start(out=win[:, s, 0:nb * 1024], in_=wsc_view(off, BLK * nb)),
            reads=all_wsc if j < NWIN else (), writes=[("win", s)], dma_key=("win", s))

    def issue_wout(j, s, off):
        add("sp", lambda e: e.dma_start(out=wout[:, s, :], in_=wsc_view(off, 128 * 2048)),
            reads=all_wsc if j < 2 else (), writes=[("wout", s)], dma_key=("wout", s))

    def issue_wg(j, s, off):
        add("sp", lambda e: e.dma_start(out=wgb[:, s, :], in_=wsc_view(off, 128 * 1024)),
            reads=all_wsc if j < 2 else (), writes=[("wg", s)], dma_key=("wg", s))

    def issue_wp(j, s, off):
        add("sp", lambda e: e.dma_start(out=wpb[:, s, :], in_=wsc_view(off, 128 * 2048)),
            reads=all_wsc if j < 2 else (), writes=[("wp", s)], dma_key=("wp", s))

    def issue_p(j, s, pc):
        ti, l = pc
        add("sp", lambda e: e.dma_start(out=pbuf[:],
                                        in_=p_d[l, :, :, ti * T:(ti + 1) * T].rearrange("k p t -> p k t")),
            writes=["pbuf"], dma_key="pld")

    S_win = Stream(sch, "win", NWIN, win_pieces, issue_win)
    S_wout = Stream(sch, "wout", 2, wout_pieces, issue_wout)
    S_wg = Stream(sch, "wg", 2, wg_pieces, issue_wg)
    S_wp = Stream(sch, "wp", 2, wp_pieces, issue_wp)
    S_p = Stream(sch, "p", 1, p_pieces, issue_p)

    rot = {"f": 0, "fl": 0, "b": 0, "y": 0, "dg": 0, "psm": 0, "psc": 0}

    NFL = 6

    def fnew():
        i = NFL + rot["f"] % (NF - NFL)
        rot["f"] += 1
        return i

    def fnew_long():
        i = rot["fl"] % NFL
        rot["fl"] += 1
        return i

    def bnew():
        i = rot["b"] % 8
        rot["b"] += 1
        return i

    def psm_new():
        i = rot["psm"] % 4
        rot["psm"] += 1
        return i

    def psc_new():
        i = 4 + rot["psc"] % 2
        rot["psc"] += 1
        return i

    PS_SUM, PS_SQ = 6, 7

    def mm(out, lhsT, rhs, start, stop, reads, bank, tp=None):
        if tp is None:
            add("pe", lambda e: e.matmul(out, lhsT, rhs, start=start, stop=stop), reads=reads, writes=[("ps", bank)])
        else:
            add("pe", lambda e: e.matmul(out, lhsT, rhs, start=start, stop=stop, tile_position=tp),
                reads=reads, writes=[("ps", bank)])

    rms_state = {"n": 0, "pend": None}

    def rms_flush():
        pd = rms_state["pend"]
        if pd is not None:
            b, first, last = pd
            mm(ps[PS_SUM][:], onesD[:], bs[:, b, :], first, last, [("bs", b), "onesD"], PS_SUM)
            rms_state["pend"] = None

    def rms_partial(kc):
        rms_flush()
        b = bnew()
        add("act", lambda e: e.activation(out=bs[:, b, :], in_=xb[:, kc, :], func=AF.Square),
            reads=[("x", kc)], writes=[("bs", b)])
        n = rms_state["n"]
        rms_state["pend"] = (b, n == 0, n == KC - 1)
        rms_state["n"] = (n + 1) % KC

    def rms_finalize():
        rms_flush()
        assert rms_state["n"] == 0
        add("act", lambda e: e.activation(out=st[:, 2, :], in_=ps[PS_SUM][:], func=AF.Ln, bias=epsb[:], scale=1.0),
            reads=[("ps", PS_SUM), "epsb"], writes=[("st", 2)])
        add("act", lambda e: e.activation(out=st[:, 2, :], in_=st[:, 2, :], func=AF.Exp, scale=-0.5),
            reads=[("st", 2)], writes=[("st", 2)])

    def h_from_x(gname, l):
        rms_finalize()
        for kc in range(KC):
            eng = "dve"
            add(eng, lambda e, kc=kc: e.scalar_tensor_tensor(
                out=hb[:, kc, :], in0=xb[:, kc, :], scalar=prm_col(gname, l * KC + kc), in1=st[:, 2, :],
                op0=ALU.mult, op1=ALU.mult),
                reads=[("x", kc), ("st", 2), "prm"], writes=[("h", kc)])

    def proj_block(bank, slot, blk):
        for kc in range(KC):
            o = blk * 1024 + kc * 128
            mm(ps[bank][:], win[:, slot, o:o + 128], hb[:, kc, :], kc == 0, kc == KC - 1,
               [("h", kc), ("win", slot)], bank)

    def pipeline(n, stages):
        ns = len(stages)
        for i in range(n + ns - 1):
            for k, stg in enumerate(stages):
                c = i - k
                if 0 <= c < n:
                    stg(c)

    def layer(ti, l, widx, last_layer):
        j = l // 2
        conv = (l % 2 == 0)
        seq_start = (ti % tiles_per_seq == 0)
        h_from_x("ng", l)
        cst_ = [dict() for _ in range(EC)]

        def stats_prep(c):
            b1, b2 = bnew(), bnew()
            add("act", lambda e: e.activation(out=bs[:, b1, :], in_=cbuf[:, c, :], func=AF.Square),
                reads=[("cbuf", c)], writes=[("bs", b1)])
            add("dve", lambda e: e.tensor_copy(out=bs[:, b2, :], in_=cbuf[:, c, :]),
                reads=[("cbuf", c)], writes=[("bs", b2)])
            cst_[c]["b1"], cst_[c]["b2"] = b1, b2

        def stats_mm(c):
            b1, b2 = cst_[c]["b1"], cst_[c]["b2"]
            mm(ps[PS_SUM][:], onesE[:], bs[:, b2, :], c == 0, c == EC - 1, [("bs", b2), "onesE"], PS_SUM)
            mm(ps[PS_SQ][:], onesE[:], bs[:, b1, :], c == 0, c == EC - 1, [("bs", b1), "onesE"], PS_SQ)

        if conv:
            def sA(c):
                slot = S_win.get(widx[0])
                widx[0] += 1
                ba, bb_ = psm_new(), psm_new()
                proj_block(ba, slot, 0)
                proj_block(bb_, slot, 1)
                f = fnew()
                add("act", lambda e: e.activation(out=fs[:, f, :], in_=ps[bb_][:], func=AF.Sigmoid),
                    reads=[("ps", bb_)], writes=[("fs", f)])
                y = rot["y"] % 2
                rot["y"] += 1
                if seq_start:
                    add("pool", lambda e: e.memset(ybuf[:, y, 0:HALO], 0.0), writes=[("yh", y)])
                else:
                    add("pool", lambda e: e.tensor_copy(out=ybuf[:, y, 0:HALO], in_=halo[:, j, c, :]),
                        reads=[("halo", j, c)], writes=[("yh", y)])
                add("dve", lambda e: e.tensor_tensor(out=ybuf[:, y, HALO:HALO + T], in0=ps[ba][:],
                                                     in1=fs[:, f, :], op=ALU.mult),
                    reads=[("ps", ba), ("fs", f)], writes=[("y", y)])
                add("pool", lambda e: e.tensor_copy(out=halo[:, j, c, :], in_=ybuf[:, y, T:T + HALO]),
                    reads=[("y", y)], writes=[("halo", j, c)])
                dg = rot["dg"] % 3
                rot["dg"] += 1
                co = (j * EC + c) * 32
                add("dve", lambda e: e.tensor_tensor(
                    out=w4[:, dg, :, :], in0=cw4[:, co:co + 32].unsqueeze(2).to_broadcast([128, 32, 32]),
                    in1=mask32[:].unsqueeze(1).to_broadcast([128, 32, 32]), op=ALU.mult),
                    reads=["cw4", "mask32"], writes=[("w4", dg)])
                par = (rot["dg"] // 3) % 2
                add("sp", lambda e: e.dma_start(out=ybd[dg], in_=ybuf[:, y, :]),
                    reads=[("y", y), ("yh", y), "ypad"], writes=[("ybd", dg)], dma_key=("ybd", dg))
                for q in range(4):
                    add("sp", lambda e, q=q: e.dma_start(
                        out=y4[32 * q:32 * q + 32, dg, :, :],
                        in_=ybd[dg, :, 8 * q:8 * q + T + 8].rearrange("(g c) u -> c g u", g=4)),
                        reads=[("ybd", dg)], writes=[("y4", dg, q)], dma_key=("y4", dg, par))
                cst_[c]["y"], cst_[c]["dg"] = y, dg

            def sB(c):
                y, dg = cst_[c]["y"], cst_[c]["dg"]
                bc = psc_new()
                y4keys = [("y4", dg, q) for q in range(4)]
                for r in range(8):
                    for g in range(4):
                        mm(ps[bc][32 * g:32 * g + 32, :], w4[:, dg, g * 8 + r, :], y4[:, dg, g, r:r + T], r == 0, r == 7,
                           [("w4", dg)] + y4keys, bc, tp=(0, 32 * g))
                cwo = PRM["cw"] + (j * EC + c) * KW
                if KD > 0:
                    fa = fnew()
                    k0 = DTAPS[0]
                    add("dve", lambda e: e.tensor_scalar(out=fs[:, fa, :], in0=ybuf[:, y, k0:k0 + T], scalar1=prm[:, cwo + k0:cwo + k0 + 1],
                                                         scalar2=None, op0=ALU.mult),
                        reads=[("y", y), ("yh", y), "prm"], writes=[("fs", fa)])
                    for k in DTAPS[1:]:
                        add("dve", lambda e, k=k: e.scalar_tensor_tensor(
                            out=fs[:, fa, :], in0=ybuf[:, y, k:k + T], scalar=prm[:, cwo + k:cwo + k + 1], in1=fs[:, fa, :],
                            op0=ALU.mult, op1=ALU.add),
                            reads=[("y", y), ("yh", y), "prm", ("fs", fa)], writes=[("fs", fa)])
                    add("dve", lambda e: e.scalar_tensor_tensor(
                        out=cbuf[:, c, :], in0=ps[bc][:], scalar=prm_col("cb", j * EC + c), in1=fs[:, fa, :],
                        op0=ALU.add, op1=ALU.add),
                        reads=[("ps", bc), "prm", ("fs", fa)], writes=[("cbuf", c)])
                else:
                    add("act", lambda e: e.activation(out=cbuf[:, c, :], in_=ps[bc][:], func=AF.Identity,
                                                      bias=prm_col("cb", j * EC + c), scale=1.0),
                        reads=[("ps", bc), "prm"], writes=[("cbuf", c)])
                stats_prep(c)

            pipeline(EC, [sA, (lambda c: None), sB, stats_mm])
        else:
            def sA(c):
                slot = S_win.get(widx[0])
                widx[0] += 1
                bb_ = psm_new()
                proj_block(bb_, slot, 0)
                add("act", lambda e: e.activation(out=cbuf[:, c, :], in_=ps[bb_][:], func=AF.Gelu),
                    reads=[("ps", bb_)], writes=[("cbuf", c)])
                stats_prep(c)

            pipeline(EC, [sA, stats_mm])
        add("act", lambda e: e.activation(out=st[:, 3, :], in_=ps[PS_SUM][:], func=AF.Square),
            reads=[("ps", PS_SUM)], writes=[("st", 3)])
        add("dve", lambda e: e.tensor_tensor(out=st[:, 1, :], in0=ps[PS_SQ][:], in1=st[:, 3, :], op=ALU.subtract),
            reads=[("ps", PS_SQ), ("st", 3)], writes=[("st", 1)])
        add("act", lambda e: e.activation(out=st[:, 1, :], in_=st[:, 1, :], func=AF.Ln, bias=epsb[:], scale=1.0),
            reads=[("st", 1), "epsb"], writes=[("st", 1)])
        add("act", lambda e: e.activation(out=st[:, 1, :], in_=st[:, 1, :], func=AF.Exp, scale=-0.5),
            reads=[("st", 1)], writes=[("st", 1)])
        add("dve", lambda e: e.scalar_tensor_tensor(out=st[:, 0, :], in0=ps[PS_SUM][:], scalar=-1.0, in1=st[:, 1, :],
                                                    op0=ALU.mult, op1=ALU.mult),
            reads=[("ps", PS_SUM), ("st", 1)], writes=[("st", 0)])
        gname, bname = ("clg", "clb") if conv else ("slg", "slb")

        def ln_apply(c):
            add("pool", lambda e: e.tensor_tensor(out=cbuf[:, c, :], in0=cbuf[:, c, :], in1=st[:, 1, :], op=ALU.mult),
                reads=[("cbuf", c), ("st", 1)], writes=[("cbuf", c)])
            add("dve", lambda e: e.tensor_tensor(out=cbuf[:, c, :], in0=cbuf[:, c, :], in1=st[:, 0, :], op=ALU.add),
                reads=[("cbuf", c), ("st", 0)], writes=[("cbuf", c)])

        if conv:
            def s2a(c):
                slot = S_win.get(widx[0])
                widx[0] += 1
                bz = psm_new()
                proj_block(bz, slot, 0)
                f3 = fnew_long()
                add("act", lambda e: e.activation(out=fs[:, f3, :], in_=ps[bz][:], func=AF.Silu),
                    reads=[("ps", bz)], writes=[("fs", f3)])
                cst_[c]["f3"] = f3

            def s2b(c):
                ln_apply(c)

            def s2c(c):
                f3 = cst_[c]["f3"]
                gcol, bcol = prm_col(gname, j * EC + c), prm_col(bname, j * EC + c)
                f2 = fnew()
                add("act", lambda e: e.activation(out=fs[:, f2, :], in_=cbuf[:, c, :], func=AF.Silu, bias=bcol, scale=gcol),
                    reads=[("cbuf", c), "prm"], writes=[("fs", f2)])
                add("dve", lambda e: e.tensor_tensor(out=ub[:, c, :], in0=fs[:, f2, :], in1=fs[:, f3, :], op=ALU.mult),
                    reads=[("fs", f2), ("fs", f3)], writes=[("ub", c)])

            pipeline(EC, [s2a, s2b, s2c])
        else:
            def s2a(c):
                slot = S_win.get(widx[0])
                widx[0] += 1
                ba, bz = psm_new(), psm_new()
                proj_block(ba, slot, 0)
                proj_block(bz, slot, 1)
                f2, f3 = fnew_long(), fnew()
                add("act", lambda e: e.activation(out=fs[:, f2, :], in_=ps[ba][:], func=AF.Gelu),
                    reads=[("ps", ba)], writes=[("fs", f2)])
                add("act", lambda e: e.activation(out=fs[:, f3, :], in_=ps[bz][:], func=AF.Silu),
                    reads=[("ps", bz)], writes=[("fs", f3)])
                add("pool", lambda e: e.tensor_tensor(out=fs[:, f2, :], in0=fs[:, f2, :], in1=fs[:, f3, :], op=ALU.mult),
                    reads=[("fs", f2), ("fs", f3)], writes=[("fs", f2)])
                cst_[c]["f2"] = f2

            def sL(c):
                ln_apply(c)

            def sV(c):
                gcol, bcol = prm_col(gname, j * EC + c), prm_col(bname, j * EC + c)
                bv = bnew()
                add("act", lambda e: e.activation(out=bs[:, bv, :], in_=cbuf[:, c, :], func=AF.Identity, bias=bcol, scale=gcol),
                    reads=[("cbuf", c), "prm"], writes=[("bs", bv)])
                cst_[c]["bv"] = bv

            def s2b2(c):
                bv = cst_[c]["bv"]
                bt = psc_new()
                for n in range(4):
                    mm(ps[bt][:, n * 128:(n + 1) * 128], bs[:, bv, n * 128:(n + 1) * 128], ident[:], True, True,
                       [("bs", bv), "ident"], bt)
                bT = bnew()
                add("dve", lambda e: e.tensor_copy(out=bs[:, bT, :], in_=ps[bt][:]),
                    reads=[("ps", bt)], writes=[("bs", bT)])
                cst_[c]["bT"] = bT

            def s2c(c):
                bT, f2 = cst_[c]["bT"], cst_[c]["f2"]
                g = c // 2
                bm = PS_SUM + (c % 2)
                wo = (j * 8 + g) * 128
                for n in range(4):
                    mm(ps[bm][:, n * 128:(n + 1) * 128], bs[:, bT, n * 128:(n + 1) * 128], wtb[:, wo:wo + 128],
                       True, True, [("bs", bT), "wtb"], bm)
                f4 = fnew()
                add("dve", lambda e: e.tensor_tensor(
                    out=fs[:, f4, :].rearrange("p (n t) -> p n t", n=4),
                    in0=ps[bm][:].rearrange("p (n t) -> p n t", n=4),
                    in1=bbc[:, wo:wo + 128].unsqueeze(1).to_broadcast([128, 4, 128]), op=ALU.add),
                    reads=[("ps", bm), "bbc"], writes=[("fs", f4)])
                add("dve", lambda e: e.tensor_tensor(out=ub[:, c, :], in0=fs[:, f4, :], in1=fs[:, f2, :], op=ALU.mult),
                    reads=[("fs", f4), ("fs", f2)], writes=[("ub", c)])

            nop = lambda c: None
            pipeline(EC, [sL, s2a, sV, s2b2, s2c])
        for dc in range(KC):
            s = S_wout.get(widx[1])
            widx[1] += 1
            bo = psm_new()
            for ec in range(EC):
                mm(ps[bo][:], wout[:, s, ec * 128:(ec + 1) * 128], ub[:, ec, :], ec == 0, ec == EC - 1,
                   [("ub", ec), ("wout", s)], bo)
            add("dve", lambda e, dc=dc, bo=bo: e.tensor_tensor(out=xb[:, dc, :], in0=xb[:, dc, :], in1=ps[bo][:], op=ALU.add),
                reads=[("x", dc), ("ps", bo)], writes=[("x", dc)])
            rms_partial(dc)
        S_p.get(widx[3])
        sp_ = S_wp.get(widx[3])
        widx[3] += 1
        add("pool", lambda e: e.tensor_copy(out=pb16[:], in_=pbuf[:]), reads=["pbuf"], writes=["pb16"])
        h_from_x("png", l)
        for dc in range(KC):
            s = S_wg.get(widx[2])
            widx[2] += 1
            bp = psc_new()
            for kc in range(2):
                o = kc * D + dc * 128
                mm(ps[bp][:], wpb[:, sp_, o:o + 128], pb16[:, kc, :], kc == 0, kc == 1, ["pb16", ("wp", sp_)], bp)
            bg = psm_new()
            for kc in range(KC):
                mm(ps[bg][:], wgb[:, s, kc * 128:(kc + 1) * 128], hb[:, kc, :], kc == 0, kc == KC - 1,
                   [("h", kc), ("wg", s)], bg)
            f = fnew()
            add("act", lambda e, f=f, bg=bg: e.activation(out=fs[:, f, :], in_=ps[bg][:], func=AF.Sigmoid),
                reads=[("ps", bg)], writes=[("fs", f)])
            add("dve", lambda e, f=f, bp=bp: e.tensor_tensor(out=fs[:, f, :], in0=fs[:, f, :], in1=ps[bp][:], op=ALU.mult),
                reads=[("fs", f), ("ps", bp)], writes=[("fs", f)])
            add("dve", lambda e, f=f, dc=dc: e.tensor_tensor(out=xb[:, dc, :], in0=xb[:, dc, :], in1=fs[:, f, :], op=ALU.add),
                reads=[("fs", f), ("x", dc)], writes=[("x", dc)])
            rms_partial(dc)

    widx = [0, 0, 0, 0]
    out_keys = []
    for kc in range(KC):
        add("sp", lambda e, kc=kc: e.dma_start(out=xb[:, kc, :], in_=x_d[kc, :, 0:T]),
            writes=[("x", kc)], dma_key=("xl", kc))
        rms_partial(kc)
    for ti in range(ntiles):
        for l in range(depth):
            layer(ti, l, widx, l == depth - 1)
        rms_finalize()
        for kc in range(KC):
            def fin(kc=kc, ti=ti):
                add("dve", lambda e: e.scalar_tensor_tensor(
                    out=cbuf[:, kc, :], in0=xb[:, kc, :], scalar=prm_col("fg", kc), in1=st[:, 2, :],
                    op0=ALU.mult, op1=ALU.mult),
                    reads=[("x", kc), ("st", 2), "prm"], writes=[("cbuf", kc)])
                add("act", lambda e: e.dma_start(out=out_d[kc, :, ti * T:(ti + 1) * T], in_=cbuf[:, kc, :]),
                    reads=[("cbuf", kc)], writes=[("out", ti, kc)], dma_key=("ost", kc))
                if ("ost", kc) not in out_keys:
                    out_keys.append(("ost", kc))
                if ti + 1 < ntiles:
                    add("sp", lambda e: e.dma_start(out=xb[:, kc, :], in_=x_d[kc, :, (ti + 1) * T:(ti + 2) * T]),
                        writes=[("x", kc)], dma_key=("xl", kc))
                    rms_partial(kc)
            fin()

    sch.emit(nc, out_keys)
    es.close()
    return nc


_NC_CACHE = {}


def kernel(x, p, norm_g, w_in, w_out, conv_w, conv_b, conv_ln_g, conv_ln_b,
           sgu_ln_g, sgu_ln_b, sgu_w, sgu_b, pl_norm_g, pl_gate_w, pl_proj_w, final_g):
    x = np.asarray(x, np.float32)
    p = np.asarray(p, np.float32)
    B, S, _ = x.shape
    bpc = B // NCORES
    ntok = bpc * S
    ntiles = ntok // T
    wts = _host_weights(np.asarray(w_in, np.float32), np.asarray(w_out, np.float32),
                        np.asarray(pl_gate_w, np.float32), np.asarray(pl_proj_w, np.float32),
                        np.asarray(sgu_w, np.float32)).reshape(NW // PP, 128, 2048)
    prm = _host_params(*[np.asarray(a, np.float32) for a in
                         (norm_g, pl_norm_g, final_g, conv_w, conv_b, conv_ln_g, conv_ln_b, sgu_ln_g, sgu_ln_b)])
    sgb = np.ascontiguousarray(np.asarray(sgu_b, np.float32).reshape(1, -1))
    cw4 = _host_cw4(np.asarray(conv_w, np.float32))
    in_maps = []
    for c in range(NCORES):
        xc = x[c * bpc:(c + 1) * bpc].reshape(ntok, D)
        x_t = np.ascontiguousarray(xc.T).reshape(KC, 128, ntok)
        pc = p[:, c * bpc:(c + 1) * bpc].reshape(DEPTH, ntok, PLE)
        p_t = np.ascontiguousarray(pc.transpose(0, 2, 1)).reshape(DEPTH, 2, 128, ntok)
        in_maps.append({"x_t": x_t, "p_t": p_t, "wts": wts, "prm": prm, "sgb": sgb, "cw4": cw4})
    nc = build_program(ntiles, DEPTH, S // T)
    res = run_bass_kernel_spmd(nc, in_maps, core_ids=list(range(NCORES)))
    out = np.empty((B, S, D), np.float32)
    for c in range(NCORES):
        o = np.asarray(res.results[c]["out_t"]).reshape(D, ntok)
        out[c * bpc:(c + 1) * bpc] = o.T.reshape(bpc, S, D)
    return out
```

```python
import numpy as np
import concourse.bass as bass
import concourse.mybir as mybir
from concourse.bass_utils import run_bass_kernel_spmd

F32 = mybir.dt.float32
BF16 = mybir.dt.bfloat16
AF = mybir.ActivationFunctionType
ALU = mybir.AluOpType

D = 1024
E = 2048
DEPTH = 4
KW = 31
PLE = 256
NCORES = 8
SEQ = 4096
T = 512
EPS = 1e-6
KC = D // 128
EC = E // 128
HALO = KW - 1
KD = 0
DTAPS = [2 * i for i in range(KD)]
PTAPS = [k for k in range(KW) if k not in DTAPS]

BLK = 128 * 1024
PP = 128 * 2048


def _win_pieces(l):
    if l % 2 == 0:
        p1 = [[(0, c), (1, c)] for c in range(EC)]
        p2 = [[(2, c)] for c in range(EC)]
    else:
        p1 = [[(1, c)] for c in range(EC)]
        p2 = [[(0, c), (2, c)] for c in range(EC)]
    return p1, p2


def _weight_offsets():
    off = 0
    win = {}
    for l in range(DEPTH):
        p1, p2 = _win_pieces(l)
        for ph, pcs in ((1, p1), (2, p2)):
            for c, blks in enumerate(pcs):
                win[(l, ph, c)] = (off, len(blks))
                off += BLK * len(blks)
    wout = {}
    for l in range(DEPTH):
        for dc in range(KC):
            wout[(l, dc)] = off
            off += 128 * EC * 128
    wg = {}
    for l in range(DEPTH):
        for dc in range(KC):
            wg[(l, dc)] = off
            off += 128 * KC * 128
    wp = {}
    for l in range(DEPTH):
        wp[l] = off
        off += 128 * 2 * D
    wsgu = off
    off += 128 * 2 * 8 * 128
    assert off % PP == 0
    return win, wout, wg, wp, wsgu, off


WIN_OFF, WOUT_OFF, WG_OFF, WP_OFF, WSGU_OFF, NW = _weight_offsets()

PRM = {}
_o = 0
for _n, _sz in (("ng", DEPTH * KC), ("png", DEPTH * KC), ("fg", KC), ("cw", 2 * EC * KW),
                ("cb", 2 * EC), ("clg", 2 * EC), ("clb", 2 * EC), ("slg", 2 * EC), ("slb", 2 * EC)):
    PRM[_n] = _o
    _o += _sz
NPRM = _o


def _host_weights(w_in, w_out, pl_gate_w, pl_proj_w, sgu_w):
    parts = []
    for l in range(DEPTH):
        wl = w_in[l].reshape(KC, 128, 3, EC, 128)
        p1, p2 = _win_pieces(l)
        for pcs in (p1, p2):
            for blks in pcs:
                bl = [wl[:, :, j, c, :].transpose(1, 0, 2).reshape(128, 1024) for (j, c) in blks]
                parts.append(np.ascontiguousarray(np.stack(bl, axis=1)).reshape(-1))
    for l in range(DEPTH):
        wl = w_out[l].reshape(EC, 128, KC, 128)
        parts.append(np.ascontiguousarray(wl.transpose(2, 1, 0, 3)).reshape(-1))
    for l in range(DEPTH):
        wl = pl_gate_w[l].reshape(KC, 128, KC, 128)
        parts.append(np.ascontiguousarray(wl.transpose(2, 1, 0, 3)).reshape(-1))
    for l in range(DEPTH):
        wl = pl_proj_w[l].reshape(2, 128, D)
        parts.append(np.ascontiguousarray(wl.transpose(1, 0, 2)).reshape(-1))
    parts.append(np.ascontiguousarray(sgu_w.transpose(3, 0, 1, 2)).reshape(-1))
    flat = np.concatenate(parts).astype(np.float32, copy=False)
    assert flat.size == NW
    return flat


def _host_cw4(conv_w):
    wpad = np.zeros((2, 32, E), np.float32)
    wpad[:, :KW] = conv_w
    a = wpad.reshape(2, 4, 8, EC, 4, 32)
    return np.ascontiguousarray(a.transpose(1, 5, 0, 3, 4, 2)).reshape(128, 2 * EC * 32)


def _host_params(norm_g, pl_norm_g, final_g, conv_w, conv_b, conv_ln_g, conv_ln_b, sgu_ln_g, sgu_ln_b):
    prm = np.zeros((128, NPRM), np.float32)

    def put(name, arr):
        prm[:, PRM[name]:PRM[name] + arr.shape[1]] = arr

    put("ng", norm_g.reshape(DEPTH, KC, 128).transpose(2, 0, 1).reshape(128, -1))
    put("png", pl_norm_g.reshape(DEPTH, KC, 128).transpose(2, 0, 1).reshape(128, -1))
    put("fg", final_g.reshape(KC, 128).T)
    put("cw", conv_w.reshape(2, KW, EC, 128).transpose(3, 0, 2, 1).reshape(128, -1))
    for nm, a in (("cb", conv_b), ("clg", conv_ln_g), ("clb", conv_ln_b), ("slg", sgu_ln_g), ("slb", sgu_ln_b)):
        put(nm, a.reshape(2, EC, 128).transpose(2, 0, 1).reshape(128, -1))
    return prm


class Op:
    __slots__ = ("eng", "fn", "deps", "signal", "is_dma", "dma_key", "sig")

    def __init__(self, eng, fn, is_dma, dma_key):
        self.eng = eng
        self.fn = fn
        self.deps = []
        self.signal = is_dma
        self.is_dma = is_dma
        self.dma_key = dma_key
        self.sig = None


class Sched:
    ENGS = ("pe", "act", "dve", "pool", "sp")

    def __init__(self):
        self.ops = {e: [] for e in self.ENGS}
        self.last_w = {}
        self.readers = {}
        self.frozen = set()
        self.dma_keys = []

    def add(self, eng, fn, reads=(), writes=(), dma_key=None):
        is_dma = dma_key is not None
        op = Op(eng, fn, is_dma, dma_key)
        if is_dma and dma_key not in self.dma_keys:
            self.dma_keys.append(dma_key)
        deps = {}
        for k in reads:
            w = self.last_w.get(k)
            if w is not None:
                deps[id(w)] = w
        for k in writes:
            assert k not in self.frozen, k
            w = self.last_w.get(k)
            if w is not None:
                deps[id(w)] = w
            for r in self.readers.get(k, ()):
                deps[id(r)] = r
        for d in deps.values():
            if d.eng == "pe" and eng == "pe" and not d.is_dma and not is_dma:
                continue
            op.deps.append(d)
            d.signal = True
        for k in reads:
            if k not in self.frozen:
                self.readers.setdefault(k, []).append(op)
        for k in writes:
            self.last_w[k] = op
            self.readers[k] = []
        self.ops[eng].append(op)
        return op

    def freeze(self, *keys):
        self.frozen.update(keys)

    def emit(self, nc, final_wait_keys):
        import contextlib
        with contextlib.ExitStack() as es:
            esem = {e: es.enter_context(nc.semaphore("s_" + e)) for e in self.ENGS}
            dsem = {k: es.enter_context(nc.semaphore("d_" + "".join(ch for ch in str(k) if ch.isalnum()))) for k in self.dma_keys}
            cnt = {e: 0 for e in self.ENGS}
            dcnt = {k: 0 for k in self.dma_keys}
            for e in self.ENGS:
                for op in self.ops[e]:
                    if op.is_dma:
                        dcnt[op.dma_key] += 16
                        op.sig = (dsem[op.dma_key], dcnt[op.dma_key])
                    elif op.signal:
                        cnt[e] += 1
                        op.sig = (esem[e], cnt[e])
            block = es.enter_context(nc.Block())

            def run(engname, engobj, final=False):
                waited = {}
                for op in self.ops[engname]:
                    need = {}
                    for d in op.deps:
                        s, v = d.sig
                        if waited.get(id(s), 0) >= v:
                            continue
                        if need.get(id(s), (None, 0))[1] < v:
                            need[id(s)] = (s, v)
                    for s, v in need.values():
                        engobj.wait_ge(s, v)
                        waited[id(s)] = v
                    ins = op.fn(engobj)
                    if op.sig is not None:
                        ins.then_inc(op.sig[0], 16 if op.is_dma else 1)
                if final:
                    for k in final_wait_keys:
                        engobj.wait_ge(dsem[k], dcnt[k])

            @block.tensor
            def _(e):
                run("pe", e)

            @block.scalar
            def _(e):
                run("act", e)

            @block.vector
            def _(e):
                run("dve", e)

            @block.gpsimd
            def _(e):
                run("pool", e)

            @block.sync
            def _(e):
                run("sp", e, final=True)


class Stream:
    def __init__(self, sch, name, nslots, pieces, issue):
        self.sch, self.name, self.n, self.pieces, self.issue = sch, name, nslots, pieces, issue
        self.next = 0

    def get(self, i):
        hi = min(len(self.pieces), i + self.n)
        while self.next < hi:
            j = self.next
            self.issue(j, j % self.n, self.pieces[j])
            self.next += 1
        return i % self.n


def build_program(ntiles, depth=DEPTH, tiles_per_seq=SEQ // T, do_prepass=True):
    ntok = ntiles * T
    nc = bass.Bass("TRN2", target_bir_lowering=False)
    x_d = nc.dram_tensor("x_t", [KC, 128, ntok], F32, kind="ExternalInput").ap()
    p_d = nc.dram_tensor("p_t", [DEPTH, 2, 128, ntok], F32, kind="ExternalInput").ap()
    w_d = nc.dram_tensor("wts", [NW // PP, 128, 2048], F32, kind="ExternalInput").ap()
    prm_d = nc.dram_tensor("prm", [128, NPRM], F32, kind="ExternalInput").ap()
    sgb_d = nc.dram_tensor("sgb", [1, 2 * 8 * 128], F32, kind="ExternalInput").ap()
    cw4_d = nc.dram_tensor("cw4", [128, 2 * EC * 32], F32, kind="ExternalInput").ap()
    out_d = nc.dram_tensor("out_t", [KC, 128, ntok], F32, kind="ExternalOutput").ap()
    wsc = nc.dram_tensor("wsc", [NW], BF16, kind="Internal").ap()
    wsc_pp = wsc.rearrange("(n p f) -> n p f", p=128, f=2048)
    ybd = nc.dram_tensor("ybd", [3, 128, T + HALO + 2], BF16, kind="Internal").ap()

    import contextlib
    es = contextlib.ExitStack()

    def sb(name, shape, dt):
        return es.enter_context(nc.sbuf_tensor(name, shape, dt))

    NWIN = 4
    NF = 12
    xb = sb("xb", [128, KC, T], F32)
    hb = sb("hb", [128, KC, T], BF16)
    cbuf = sb("cbuf", [128, EC, T], F32)
    ub = sb("ub", [128, EC, T], BF16)
    YW = T + HALO + 2
    ybuf = sb("ybuf", [128, 2, YW], BF16)
    halo = sb("halo", [128, 2, EC, HALO], BF16)
    y4 = sb("y4", [128, 3, 4, T + 8], BF16)
    w4 = sb("w4", [128, 3, 32, 32], BF16)
    mask32 = sb("mask32", [128, 32], BF16)
    cw4 = sb("cw4_sb", [128, 2 * EC * 32], F32)
    fs = sb("fs", [128, NF, T], F32)
    bs = sb("bs", [128, 8, T], BF16)
    st = sb("st", [128, 4, T], F32)
    pbuf = sb("pbuf", [128, 2, T], F32)
    pb16 = sb("pb16", [128, 2, T], BF16)
    win = sb("win", [128, NWIN, 2048], BF16)
    wout = sb("wout", [128, 2, EC * 128], BF16)
    wgb = sb("wgb", [128, 2, KC * 128], BF16)
    wpb = sb("wpb", [128, 2, 2 * D], BF16)
    wtb = sb("wtb", [128, 2 * 8 * 128], BF16)
    bbc = sb("bbc", [128, 2 * 8 * 128], F32)
    prm = sb("prm_sb", [128, NPRM], F32)
    ident = sb("ident", [128, 128], BF16)
    onesD = sb("onesD", [128, 128], BF16)
    onesE = sb("onesE", [128, 128], BF16)
    onesf = sb("onesf", [128, 128], F32)
    epsb = sb("epsb", [128, 1], F32)
    ps = [es.enter_context(nc.psum_tensor("ps%d" % i, [128, T], F32)) for i in range(8)]

    sch = Sched()
    add = sch.add

    def prm_col(name, idx):
        o = PRM[name] + idx
        return prm[:, o:o + 1]

    add("sp", lambda e: e.dma_start(out=prm[:], in_=prm_d), writes=["prm"], dma_key="setup")
    add("sp", lambda e: e.dma_start(out=bbc[:], in_=sgb_d.partition_broadcast(128)), writes=["bbc"], dma_key="setup2")
    add("pool", lambda e: e.memset(onesf[:], 1.0), writes=["onesf"])
    add("pool", lambda e: e.memset(onesD[:], 1.0 / D), writes=["onesD"])
    add("pool", lambda e: e.memset(epsb[:], EPS), writes=["epsb"])
    add("pool", lambda e: e.memset(onesE[:], 1.0 / E), writes=["onesE"])
    add("pool", lambda e: e.affine_select(out=ident[:], in_=onesf[:], pattern=[[-1, 128]],
                                          compare_op=ALU.is_equal, fill=0.0, base=0, channel_multiplier=1),
        reads=["onesf"], writes=["ident"])
    add("sp", lambda e: e.dma_start(out=cw4[:], in_=cw4_d), writes=["cw4"], dma_key="setup4")
    add("pool", lambda e: e.tensor_tensor(out=mask32[:], in0=ident[:, 0:32], in1=ident[:, 32:64], op=ALU.add),
        reads=["ident"], writes=["mask32"])
    add("pool", lambda e: e.tensor_tensor(out=mask32[:], in0=mask32[:], in1=ident[:, 64:96], op=ALU.add),
        reads=["ident", "mask32"], writes=["mask32"])
    add("pool", lambda e: e.tensor_tensor(out=mask32[:], in0=mask32[:], in1=ident[:, 96:128], op=ALU.add),
        reads=["ident", "mask32"], writes=["mask32"])
    add("pool", lambda e: e.memset(ybuf[:, :, T + HALO:YW], 0.0), writes=["ypad"])
    sch.freeze("prm", "bbc", "onesD", "onesE", "ident", "onesf", "epsb", "cw4", "mask32", "ypad")

    cst = cbuf[:].rearrange("p (s c) t -> p s (c t)", s=4)
    ust = ub[:].rearrange("p (s c) t -> p s (c t)", s=4)
    npp = NW // PP
    if do_prepass:
        cast_engs = ("act", "dve")
        for i in range(npp):
            s = i % 4
            ckeys = [("cbuf", 4 * s + q) for q in range(4)]
            ukeys = [("ub", 4 * s + q) for q in range(4)]
            add("sp", lambda e, i=i, s=s: e.dma_start(out=cst[:, s, :], in_=w_d[i]),
                writes=ckeys, dma_key=("ppl", s))
            ce = cast_engs[i % 2]
            if ce == "act":
                add("act", lambda e, s=s: e.copy(out=ust[:, s, :], in_=cst[:, s, :]), reads=ckeys, writes=ukeys)
            else:
                add(ce, lambda e, s=s: e.tensor_copy(out=ust[:, s, :], in_=cst[:, s, :]), reads=ckeys, writes=ukeys)
            add("sp", lambda e, i=i, s=s: e.dma_start(out=wsc_pp[i], in_=ust[:, s, :]),
                reads=ukeys, writes=[("wsc", i)], dma_key=("pps", s))
    all_wsc = [("wsc", i) for i in range(npp)]

    def wsc_view(off, n):
        return wsc[off:off + n].rearrange("(p f) -> p f", p=128)

    add("sp", lambda e: e.dma_start(out=wtb[:], in_=wsc_view(WSGU_OFF, 128 * 2048)),
        reads=all_wsc, writes=["wtb"], dma_key="setup3")
    wtb3 = wtb[:].rearrange("p (a t) -> p a t", t=128)
    add("pool", lambda e: e.affine_select(out=wtb3, in_=wtb3, pattern=[[0, 16], [1, 128]],
                                          compare_op=ALU.is_ge, fill=0.0, base=0, channel_multiplier=-1),
        reads=["wtb"], writes=["wtb"])
    sch.freeze("wtb")

    win_pieces, wout_pieces, wg_pieces, wp_pieces, p_pieces = [], [], [], [], []
    for ti in range(ntiles):
        for l in range(depth):
            for ph in (1, 2):
                for c in range(EC):
                    win_pieces.append(WIN_OFF[(l, ph, c)])
            for dc in range(KC):
                wout_pieces.append(WOUT_OFF[(l, dc)])
                wg_pieces.append(WG_OFF[(l, dc)])
            wp_pieces.append(WP_OFF[l])
            p_pieces.append((ti, l))

    def issue_win(j, s, pc):
        off, nb = pc
        add("sp", lambda e: e.dma_start(out=win[:, s, 0:nb * 1024], in_=wsc_view(off, BLK * nb)),
            reads=all_wsc if j < NWIN else (), writes=[("win", s)], dma_key=("win", s))

    def issue_wout(j, s, off):
        add("sp", lambda e: e.dma_start(out=wout[:, s, :], in_=wsc_view(off, 128 * 2048)),
            reads=all_wsc if j < 2 else (), writes=[("wout", s)], dma_key=("wout", s))

    def issue_wg(j, s, off):
        add("sp", lambda e: e.dma_start(out=wgb[:, s, :], in_=wsc_view(off, 128 * 1024)),
            reads=all_wsc if j < 2 else (), writes=[("wg", s)], dma_key=("wg", s))

    def issue_wp(j, s, off):
        add("sp", lambda e: e.dma_start(out=wpb[:, s, :], in_=wsc_view(off, 128 * 2048)),
            reads=all_wsc if j < 2 else (), writes=[("wp", s)], dma_key=("wp", s))

    def issue_p(j, s, pc):
        ti, l = pc
        add("sp", lambda e: e.dma_start(out=pbuf[:],
                                        in_=p_d[l, :, :, ti * T:(ti + 1) * T].rearrange("k p t -> p k t")),
            writes=["pbuf"], dma_key="pld")

    S_win = Stream(sch, "win", NWIN, win_pieces, issue_win)
    S_wout = Stream(sch, "wout", 2, wout_pieces, issue_wout)
    S_wg = Stream(sch, "wg", 2, wg_pieces, issue_wg)
    S_wp = Stream(sch, "wp", 2, wp_pieces, issue_wp)
    S_p = Stream(sch, "p", 1, p_pieces, issue_p)

    rot = {"f": 0, "fl": 0, "b": 0, "y": 0, "dg": 0, "psm": 0, "psc": 0}

    NFL = 6

    def fnew():
        i = NFL + rot["f"] % (NF - NFL)
        rot["f"] += 1
        return i

    def fnew_long():
        i = rot["fl"] % NFL
        rot["fl"] += 1
        return i

    def bnew():
        i = rot["b"] % 8
        rot["b"] += 1
        return i

    def psm_new():
        i = rot["psm"] % 4
        rot["psm"] += 1
        return i

    def psc_new():
        i = 4 + rot["psc"] % 2
        rot["psc"] += 1
        return i

    PS_SUM, PS_SQ = 6, 7

    def mm(out, lhsT, rhs, start, stop, reads, bank, tp=None):
        if tp is None:
            add("pe", lambda e: e.matmul(out, lhsT, rhs, start=start, stop=stop), reads=reads, writes=[("ps", bank)])
        else:
            add("pe", lambda e: e.matmul(out, lhsT, rhs, start=start, stop=stop, tile_position=tp),
                reads=reads, writes=[("ps", bank)])

    rms_state = {"n": 0, "pend": None}

    def rms_flush():
        pd = rms_state["pend"]
        if pd is not None:
            b, first, last = pd
            mm(ps[PS_SUM][:], onesD[:], bs[:, b, :], first, last, [("bs", b), "onesD"], PS_SUM)
            rms_state["pend"] = None

    def rms_partial(kc):
        rms_flush()
        b = bnew()
        add("act", lambda e: e.activation(out=bs[:, b, :], in_=xb[:, kc, :], func=AF.Square),
            reads=[("x", kc)], writes=[("bs", b)])
        n = rms_state["n"]
        rms_state["pend"] = (b, n == 0, n == KC - 1)
        rms_state["n"] = (n + 1) % KC

    def rms_finalize():
        rms_flush()
        assert rms_state["n"] == 0
        add("act", lambda e: e.activation(out=st[:, 2, :], in_=ps[PS_SUM][:], func=AF.Ln, bias=epsb[:], scale=1.0),
            reads=[("ps", PS_SUM), "epsb"], writes=[("st", 2)])
        add("act", lambda e: e.activation(out=st[:, 2, :], in_=st[:, 2, :], func=AF.Exp, scale=-0.5),
            reads=[("st", 2)], writes=[("st", 2)])

    def h_from_x(gname, l):
        rms_finalize()
        for kc in range(KC):
            eng = "dve"
            add(eng, lambda e, kc=kc: e.scalar_tensor_tensor(
                out=hb[:, kc, :], in0=xb[:, kc, :], scalar=prm_col(gname, l * KC + kc), in1=st[:, 2, :],
                op0=ALU.mult, op1=ALU.mult),
                reads=[("x", kc), ("st", 2), "prm"], writes=[("h", kc)])

    def proj_block(bank, slot, blk):
        for kc in range(KC):
            o = blk * 1024 + kc * 128
            mm(ps[bank][:], win[:, slot, o:o + 128], hb[:, kc, :], kc == 0, kc == KC - 1,
               [("h", kc), ("win", slot)], bank)

    def pipeline(n, stages):
        ns = len(stages)
        for i in range(n + ns - 1):
            for k, stg in enumerate(stages):
                c = i - k
                if 0 <= c < n:
                    stg(c)

    def layer(ti, l, widx, last_layer):
        j = l // 2
        conv = (l % 2 == 0)
        seq_start = (ti % tiles_per_seq == 0)
        h_from_x("ng", l)
        cst_ = [dict() for _ in range(EC)]

        def stats_prep(c):
            b1, b2 = bnew(), bnew()
            add("act", lambda e: e.activation(out=bs[:, b1, :], in_=cbuf[:, c, :], func=AF.Square),
                reads=[("cbuf", c)], writes=[("bs", b1)])
            add("dve", lambda e: e.tensor_copy(out=bs[:, b2, :], in_=cbuf[:, c, :]),
                reads=[("cbuf", c)], writes=[("bs", b2)])
            cst_[c]["b1"], cst_[c]["b2"] = b1, b2

        def stats_mm(c):
            b1, b2 = cst_[c]["b1"], cst_[c]["b2"]
            mm(ps[PS_SUM][:], onesE[:], bs[:, b2, :], c == 0, c == EC - 1, [("bs", b2), "onesE"], PS_SUM)
            mm(ps[PS_SQ][:], onesE[:], bs[:, b1, :], c == 0, c == EC - 1, [("bs", b1), "onesE"], PS_SQ)

        if conv:
            def sA(c):
                slot = S_win.get(widx[0])
                widx[0] += 1
                ba, bb_ = psm_new(), psm_new()
                proj_block(ba, slot, 0)
                proj_block(bb_, slot, 1)
                f = fnew()
                add("act", lambda e: e.activation(out=fs[:, f, :], in_=ps[bb_][:], func=AF.Sigmoid),
                    reads=[("ps", bb_)], writes=[("fs", f)])
                y = rot["y"] % 2
                rot["y"] += 1
                if seq_start:
                    add("pool", lambda e: e.memset(ybuf[:, y, 0:HALO], 0.0), writes=[("yh", y)])
                else:
                    add("pool", lambda e: e.tensor_copy(out=ybuf[:, y, 0:HALO], in_=halo[:, j, c, :]),
                        reads=[("halo", j, c)], writes=[("yh", y)])
                add("dve", lambda e: e.tensor_tensor(out=ybuf[:, y, HALO:HALO + T], in0=ps[ba][:],
                                                     in1=fs[:, f, :], op=ALU.mult),
                    reads=[("ps", ba), ("fs", f)], writes=[("y", y)])
                add("pool", lambda e: e.tensor_copy(out=halo[:, j, c, :], in_=ybuf[:, y, T:T + HALO]),
                    reads=[("y", y)], writes=[("halo", j, c)])
                dg = rot["dg"] % 3
                rot["dg"] += 1
                co = (j * EC + c) * 32
                add("dve", lambda e: e.tensor_tensor(
                    out=w4[:, dg, :, :], in0=cw4[:, co:co + 32].unsqueeze(2).to_broadcast([128, 32, 32]),
                    in1=mask32[:].unsqueeze(1).to_broadcast([128, 32, 32]), op=ALU.mult),
                    reads=["cw4", "mask32"], writes=[("w4", dg)])
                par = (rot["dg"] // 3) % 2
                add("sp", lambda e: e.dma_start(out=ybd[dg], in_=ybuf[:, y, :]),
                    reads=[("y", y), ("yh", y), "ypad"], writes=[("ybd", dg)], dma_key=("ybd", dg))
                for q in range(4):
                    add("sp", lambda e, q=q: e.dma_start(
                        out=y4[32 * q:32 * q + 32, dg, :, :],
                        in_=ybd[dg, :, 8 * q:8 * q + T + 8].rearrange("(g c) u -> c g u", g=4)),
                        reads=[("ybd", dg)], writes=[("y4", dg, q)], dma_key=("y4", dg, par))
                cst_[c]["y"], cst_[c]["dg"] = y, dg

            def sB(c):
                y, dg = cst_[c]["y"], cst_[c]["dg"]
                bc = psc_new()
                y4keys = [("y4", dg, q) for q in range(4)]
                for r in range(8):
                    for g in range(4):
                        mm(ps[bc][32 * g:32 * g + 32, :], w4[:, dg, g * 8 + r, :], y4[:, dg, g, r:r + T], r == 0, r == 7,
                           [("w4", dg)] + y4keys, bc, tp=(0, 32 * g))
                cwo = PRM["cw"] + (j * EC + c) * KW
                if KD > 0:
                    fa = fnew()
                    k0 = DTAPS[0]
                    add("dve", lambda e: e.tensor_scalar(out=fs[:, fa, :], in0=ybuf[:, y, k0:k0 + T], scalar1=prm[:, cwo + k0:cwo + k0 + 1],
                                                         scalar2=None, op0=ALU.mult),
                        reads=[("y", y), ("yh", y), "prm"], writes=[("fs", fa)])
                    for k in DTAPS[1:]:
                        add("dve", lambda e, k=k: e.scalar_tensor_tensor(
                            out=fs[:, fa, :], in0=ybuf[:, y, k:k + T], scalar=prm[:, cwo + k:cwo + k + 1], in1=fs[:, fa, :],
                            op0=ALU.mult, op1=ALU.add),
                            reads=[("y", y), ("yh", y), "prm", ("fs", fa)], writes=[("fs", fa)])
                    add("dve", lambda e: e.scalar_tensor_tensor(
                        out=cbuf[:, c, :], in0=ps[bc][:], scalar=prm_col("cb", j * EC + c), in1=fs[:, fa, :],
                        op0=ALU.add, op1=ALU.add),
                        reads=[("ps", bc), "prm", ("fs", fa)], writes=[("cbuf", c)])
                else:
                    add("act", lambda e: e.activation(out=cbuf[:, c, :], in_=ps[bc][:], func=AF.Identity,
                                                      bias=prm_col("cb", j * EC + c), scale=1.0),
                        reads=[("ps", bc), "prm"], writes=[("cbuf", c)])
                stats_prep(c)

            def s2a_early(c):
                slot = S_win.get(widx[0])
                widx[0] += 1
                bz = psm_new()
                proj_block(bz, slot, 0)
                f3 = fnew_long()
                add("act", lambda e: e.activation(out=fs[:, f3, :], in_=ps[bz][:], func=AF.Silu),
                    reads=[("ps", bz)], writes=[("fs", f3)])
                cst_[c]["f3"] = f3

            NEARLY = 3
            for i in range(EC + 3):
                if i < EC:
                    sA(i)
                else:
                    s2a_early(i - EC)
                if 0 <= i - 2 < EC:
                    sB(i - 2)
                if 0 <= i - 3 < EC:
                    stats_mm(i - 3)
        else:
            def sA(c):
                slot = S_win.get(widx[0])
                widx[0] += 1
                bb_ = psm_new()
                proj_block(bb_, slot, 0)
                add("act", lambda e: e.activation(out=cbuf[:, c, :], in_=ps[bb_][:], func=AF.Gelu),
                    reads=[("ps", bb_)], writes=[("cbuf", c)])
                stats_prep(c)

            pipeline(EC, [sA, stats_mm])
        add("act", lambda e: e.activation(out=st[:, 3, :], in_=ps[PS_SUM][:], func=AF.Square),
            reads=[("ps", PS_SUM)], writes=[("st", 3)])
        add("dve", lambda e: e.tensor_tensor(out=st[:, 1, :], in0=ps[PS_SQ][:], in1=st[:, 3, :], op=ALU.subtract),
            reads=[("ps", PS_SQ), ("st", 3)], writes=[("st", 1)])
        add("act", lambda e: e.activation(out=st[:, 1, :], in_=st[:, 1, :], func=AF.Ln, bias=epsb[:], scale=1.0),
            reads=[("st", 1), "epsb"], writes=[("st", 1)])
        add("act", lambda e: e.activation(out=st[:, 1, :], in_=st[:, 1, :], func=AF.Exp, scale=-0.5),
            reads=[("st", 1)], writes=[("st", 1)])
        add("dve", lambda e: e.scalar_tensor_tensor(out=st[:, 0, :], in0=ps[PS_SUM][:], scalar=-1.0, in1=st[:, 1, :],
                                                    op0=ALU.mult, op1=ALU.mult),
            reads=[("ps", PS_SUM), ("st", 1)], writes=[("st", 0)])
        gname, bname = ("clg", "clb") if conv else ("slg", "slb")

        def ln_apply(c):
            add("pool", lambda e: e.tensor_tensor(out=cbuf[:, c, :], in0=cbuf[:, c, :], in1=st[:, 1, :], op=ALU.mult),
                reads=[("cbuf", c), ("st", 1)], writes=[("cbuf", c)])
            add("dve", lambda e: e.tensor_tensor(out=cbuf[:, c, :], in0=cbuf[:, c, :], in1=st[:, 0, :], op=ALU.add),
                reads=[("cbuf", c), ("st", 0)], writes=[("cbuf", c)])

        if conv:
            def s2a(c):
                if c >= NEARLY:
                    s2a_early(c)

            def s2b(c):
                ln_apply(c)

            def s2c(c):
                f3 = cst_[c]["f3"]
                gcol, bcol = prm_col(gname, j * EC + c), prm_col(bname, j * EC + c)
                f2 = fnew()
                add("act", lambda e: e.activation(out=fs[:, f2, :], in_=cbuf[:, c, :], func=AF.Silu, bias=bcol, scale=gcol),
                    reads=[("cbuf", c), "prm"], writes=[("fs", f2)])
                add("dve", lambda e: e.tensor_tensor(out=ub[:, c, :], in0=fs[:, f2, :], in1=fs[:, f3, :], op=ALU.mult),
                    reads=[("fs", f2), ("fs", f3)], writes=[("ub", c)])

            pipeline(EC, [s2a, s2b, s2c])
        else:
            def s2a(c):
                slot = S_win.get(widx[0])
                widx[0] += 1
                ba, bz = psm_new(), psm_new()
                proj_block(ba, slot, 0)
                proj_block(bz, slot, 1)
                f2, f3 = fnew_long(), fnew()
                add("act", lambda e: e.activation(out=fs[:, f2, :], in_=ps[ba][:], func=AF.Gelu),
                    reads=[("ps", ba)], writes=[("fs", f2)])
                add("act", lambda e: e.activation(out=fs[:, f3, :], in_=ps[bz][:], func=AF.Silu),
                    reads=[("ps", bz)], writes=[("fs", f3)])
                add("pool", lambda e: e.tensor_tensor(out=fs[:, f2, :], in0=fs[:, f2, :], in1=fs[:, f3, :], op=ALU.mult),
                    reads=[("fs", f2), ("fs", f3)], writes=[("fs", f2)])
                cst_[c]["f2"] = f2

            def sL(c):
                ln_apply(c)

            def sV(c):
                gcol, bcol = prm_col(gname, j * EC + c), prm_col(bname, j * EC + c)
                bv = bnew()
                add("act", lambda e: e.activation(out=bs[:, bv, :], in_=cbuf[:, c, :], func=AF.Identity, bias=bcol, scale=gcol),
                    reads=[("cbuf", c), "prm"], writes=[("bs", bv)])
                cst_[c]["bv"] = bv

            def s2b2(c):
                bv = cst_[c]["bv"]
                bt = psc_new()
                for n in range(4):
                    mm(ps[bt][:, n * 128:(n + 1) * 128], bs[:, bv, n * 128:(n + 1) * 128], ident[:], True, True,
                       [("bs", bv), "ident"], bt)
                bT = bnew()
                add("dve", lambda e: e.tensor_copy(out=bs[:, bT, :], in_=ps[bt][:]),
                    reads=[("ps", bt)], writes=[("bs", bT)])
                cst_[c]["bT"] = bT

            def s2c(c):
                bT, f2 = cst_[c]["bT"], cst_[c]["f2"]
                g = c // 2
                bm = PS_SUM + (c % 2)
                wo = (j * 8 + g) * 128
                for n in range(4):
                    mm(ps[bm][:, n * 128:(n + 1) * 128], bs[:, bT, n * 128:(n + 1) * 128], wtb[:, wo:wo + 128],
                       True, True, [("bs", bT), "wtb"], bm)
                f4 = fnew()
                add("dve", lambda e: e.tensor_tensor(
                    out=fs[:, f4, :].rearrange("p (n t) -> p n t", n=4),
                    in0=ps[bm][:].rearrange("p (n t) -> p n t", n=4),
                    in1=bbc[:, wo:wo + 128].unsqueeze(1).to_broadcast([128, 4, 128]), op=ALU.add),
                    reads=[("ps", bm), "bbc"], writes=[("fs", f4)])
                add("dve", lambda e: e.tensor_tensor(out=ub[:, c, :], in0=fs[:, f4, :], in1=fs[:, f2, :], op=ALU.mult),
                    reads=[("fs", f4), ("fs", f2)], writes=[("ub", c)])

            nop = lambda c: None
            pipeline(EC, [sL, s2a, sV, s2b2, s2c])
        for dc in range(KC):
            s = S_wout.get(widx[1])
            widx[1] += 1
            bo = psm_new()
            for ec in range(EC):
                mm(ps[bo][:], wout[:, s, ec * 128:(ec + 1) * 128], ub[:, ec, :], ec == 0, ec == EC - 1,
                   [("ub", ec), ("wout", s)], bo)
            add("dve", lambda e, dc=dc, bo=bo: e.tensor_tensor(out=xb[:, dc, :], in0=xb[:, dc, :], in1=ps[bo][:], op=ALU.add),
                reads=[("x", dc), ("ps", bo)], writes=[("x", dc)])
            rms_partial(dc)
        S_p.get(widx[3])
        sp_ = S_wp.get(widx[3])
        widx[3] += 1
        add("pool", lambda e: e.tensor_copy(out=pb16[:], in_=pbuf[:]), reads=["pbuf"], writes=["pb16"])
        h_from_x("png", l)
        for dc in range(KC):
            s = S_wg.get(widx[2])
            widx[2] += 1
            bp = psc_new()
            for kc in range(2):
                o = kc * D + dc * 128
                mm(ps[bp][:], wpb[:, sp_, o:o + 128], pb16[:, kc, :], kc == 0, kc == 1, ["pb16", ("wp", sp_)], bp)
            bg = psm_new()
            for kc in range(KC):
                mm(ps[bg][:], wgb[:, s, kc * 128:(kc + 1) * 128], hb[:, kc, :], kc == 0, kc == KC - 1,
                   [("h", kc), ("wg", s)], bg)
            f = fnew()
            add("act", lambda e, f=f, bg=bg: e.activation(out=fs[:, f, :], in_=ps[bg][:], func=AF.Sigmoid),
                reads=[("ps", bg)], writes=[("fs", f)])
            add("dve", lambda e, f=f, bp=bp: e.tensor_tensor(out=fs[:, f, :], in0=fs[:, f, :], in1=ps[bp][:], op=ALU.mult),
                reads=[("fs", f), ("ps", bp)], writes=[("fs", f)])
            add("dve", lambda e, f=f, dc=dc: e.tensor_tensor(out=xb[:, dc, :], in0=xb[:, dc, :], in1=fs[:, f, :], op=ALU.add),
                reads=[("fs", f), ("x", dc)], writes=[("x", dc)])
            rms_partial(dc)

    widx = [0, 0, 0, 0]
    out_keys = []
    for kc in range(KC):
        add("sp", lambda e, kc=kc: e.dma_start(out=xb[:, kc, :], in_=x_d[kc, :, 0:T]),
            writes=[("x", kc)], dma_key=("xl", kc))
        rms_partial(kc)
    for ti in range(ntiles):
        for l in range(depth):
            layer(ti, l, widx, l == depth - 1)
        rms_finalize()
        for kc in range(KC):
            def fin(kc=kc, ti=ti):
                add("dve", lambda e: e.scalar_tensor_tensor(
                    out=cbuf[:, kc, :], in0=xb[:, kc, :], scalar=prm_col("fg", kc), in1=st[:, 2, :],
                    op0=ALU.mult, op1=ALU.mult),
                    reads=[("x", kc), ("st", 2), "prm"], writes=[("cbuf", kc)])
                add("act", lambda e: e.dma_start(out=out_d[kc, :, ti * T:(ti + 1) * T], in_=cbuf[:, kc, :]),
                    reads=[("cbuf", kc)], writes=[("out", ti, kc)], dma_key=("ost", kc))
                if ("ost", kc) not in out_keys:
                    out_keys.append(("ost", kc))
                if ti + 1 < ntiles:
                    add("sp", lambda e: e.dma_start(out=xb[:, kc, :], in_=x_d[kc, :, (ti + 1) * T:(ti + 2) * T]),
                        writes=[("x", kc)], dma_key=("xl", kc))
                    rms_partial(kc)
            fin()

    sch.emit(nc, out_keys)
    es.close()
    return nc


_NC_CACHE = {}


def kernel(x, p, norm_g, w_in, w_out, conv_w, conv_b, conv_ln_g, conv_ln_b,
           sgu_ln_g, sgu_ln_b, sgu_w, sgu_b, pl_norm_g, pl_gate_w, pl_proj_w, final_g):
    x = np.asarray(x, np.float32)
    p = np.asarray(p, np.float32)
    B, S, _ = x.shape
    bpc = B // NCORES
    ntok = bpc * S
    ntiles = ntok // T
    wts = _host_weights(np.asarray(w_in, np.float32), np.asarray(w_out, np.float32),
                        np.asarray(pl_gate_w, np.float32), np.asarray(pl_proj_w, np.float32),
                        np.asarray(sgu_w, np.float32)).reshape(NW // PP, 128, 2048)
    prm = _host_params(*[np.asarray(a, np.float32) for a in
                         (norm_g, pl_norm_g, final_g, conv_w, conv_b, conv_ln_g, conv_ln_b, sgu_ln_g, sgu_ln_b)])
    sgb = np.ascontiguousarray(np.asarray(sgu_b, np.float32).reshape(1, -1))
    cw4 = _host_cw4(np.asarray(conv_w, np.float32))
    in_maps = []
    for c in range(NCORES):
        xc = x[c * bpc:(c + 1) * bpc].reshape(ntok, D)
        x_t = np.ascontiguousarray(xc.T).reshape(KC, 128, ntok)
        pc = p[:, c * bpc:(c + 1) * bpc].reshape(DEPTH, ntok, PLE)
        p_t = np.ascontiguousarray(pc.transpose(0, 2, 1)).reshape(DEPTH, 2, 128, ntok)
        in_maps.append({"x_t": x_t, "p_t": p_t, "wts": wts, "prm": prm, "sgb": sgb, "cw4": cw4})
    nc = build_program(ntiles, DEPTH, S // T)
    res = run_bass_kernel_spmd(nc, in_maps, core_ids=list(range(NCORES)))
    out = np.empty((B, S, D), np.float32)
    for c in range(NCORES):
        o = np.asarray(res.results[c]["out_t"]).reshape(D, ntok)
        out[c * bpc:(c + 1) * bpc] = o.T.reshape(bpc, S, D)
    return out
```
